# Optimizing a Trainium2 kernel written in Bass

```python
import math
import jax, jax.numpy as jnp
from jax import lax
import numpy as np

D_MODEL = 1024
BATCH = 8
SEQ = 4096
DEPTH = 4

CHUNK = 64
Q_BLOCK = 128
N_MIXERS = 2
SB_HEADS = 16
SB_HEAD_DIM = D_MODEL // SB_HEADS
DIFF_HEADS = 8
DIFF_HEAD_DIM = D_MODEL // (2 * DIFF_HEADS)
DIFF_V_DIM = 2 * DIFF_HEAD_DIM
ROPE_THETA = 500000.0
ROT_DIM = DIFF_HEAD_DIM // 4
D_FF = -(-8 * D_MODEL // (3 * 256)) * 256
EPS = 1e-6
NEG_INF = -1e30
N_SB = (DEPTH + 1) // 2
N_DIFF = DEPTH // 2

kernel_name = 'hybrid_stickbreak_diffattn_swiglu'


def rms_norm(x, g):
    xf = x.astype(jnp.float32)
    y = xf * lax.rsqrt(jnp.mean(xf * xf, axis=-1, keepdims=True) + EPS)
    return (y * g.astype(jnp.float32)).astype(x.dtype)


def partial_rope(x, positions):
    half = ROT_DIM // 2
    inv_freq = ROPE_THETA ** (-jnp.arange(0, ROT_DIM, 2, dtype=jnp.float32) / ROT_DIM)
    ang = positions.astype(jnp.float32)[..., None] * inv_freq
    cos = jnp.cos(ang)[:, :, None, :]
    sin = jnp.sin(ang)[:, :, None, :]
    xr = x[..., :ROT_DIM].astype(jnp.float32)
    x1, x2 = xr[..., :half], xr[..., half:]
    rot = jnp.concatenate([x1 * cos - x2 * sin, x2 * cos + x1 * sin], axis=-1)
    return jnp.concatenate([rot.astype(x.dtype), x[..., ROT_DIM:]], axis=-1)


def _query_blocks(t):
    b, s, h, d = t.shape
    return jnp.moveaxis(t.reshape(b, s // Q_BLOCK, Q_BLOCK, h, d), 1, 0)


def _merge_blocks(o):
    nb, b, qb, h, d = o.shape
    return jnp.moveaxis(o, 0, 1).reshape(b, nb * qb, h * d)


def stick_breaking_attention(q, k, v):
    seq = k.shape[1]
    scale = SB_HEAD_DIM ** -0.5
    key_pos = jnp.arange(seq, dtype=jnp.int32)

    def block(args):
        q_blk, blk = args
        q_pos = blk * Q_BLOCK + jnp.arange(Q_BLOCK, dtype=jnp.int32)
        z = jnp.einsum('bqhd,bkhd->bhqk', q_blk, k, preferred_element_type=jnp.float32) * scale
        earlier = key_pos[None, :] < q_pos[:, None]
        log_keep = jnp.where(earlier, jax.nn.log_sigmoid(-z), 0.0)
        log_after = lax.cumsum(log_keep, axis=3, reverse=True) - log_keep
        weight = jnp.where(earlier, jnp.exp(jax.nn.log_sigmoid(z) + log_after), 0.0)
        return jnp.einsum('bhqk,bkhd->bqhd', weight.astype(v.dtype), v)

    out = lax.map(block, (_query_blocks(q), jnp.arange(seq // Q_BLOCK, dtype=jnp.int32)))
    return _merge_blocks(out)


def differential_attention(q1, q2, k1, k2, v, lam):
    seq = k1.shape[1]
    scale = DIFF_HEAD_DIM ** -0.5
    key_chunk = jnp.arange(seq, dtype=jnp.int32) // CHUNK

    def block(args):
        q1_blk, q2_blk, blk = args
        q_chunk = (blk * Q_BLOCK + jnp.arange(Q_BLOCK, dtype=jnp.int32)) // CHUNK
        visible = key_chunk[None, :] <= q_chunk[:, None]

        def attn_map(qb, kk):
            s = jnp.einsum('bqhd,bkhd->bhqk', qb, kk, preferred_element_type=jnp.float32) * scale
            return jax.nn.softmax(jnp.where(visible, s, NEG_INF), axis=-1)

        a = attn_map(q1_blk, k1) - lam * attn_map(q2_blk, k2)
        return jnp.einsum('bhqk,bkhd->bqhd', a.astype(v.dtype), v)

    out = lax.map(block, (_query_blocks(q1), _query_blocks(q2),
                          jnp.arange(seq // Q_BLOCK, dtype=jnp.int32)))
    return _merge_blocks(out)


def stick_breaking_mixer(h, w_in, w_out):
    b, s, _ = h.shape
    q, k, v = jnp.split(jnp.einsum('bsd,de->bse', h, w_in), 3, axis=-1)
    shape = (b, s, SB_HEADS, SB_HEAD_DIM)
    o = stick_breaking_attention(q.reshape(shape), k.reshape(shape), v.reshape(shape))
    return jnp.einsum('bse,ed->bsd', o, w_out)


def diff_mixer(h, positions, w_in, w_out, q_g, k_g, lq1, lk1, lq2, lk2, sub_g, layer_idx):
    b, s, _ = h.shape
    q, k, v = jnp.split(jnp.einsum('bsd,de->bse', h, w_in), 3, axis=-1)
    q = rms_norm(q.reshape(b, s, DIFF_HEADS, 2, DIFF_HEAD_DIM), q_g)
    k = rms_norm(k.reshape(b, s, DIFF_HEADS, 2, DIFF_HEAD_DIM), k_g)
    q1 = partial_rope(q[..., 0, :], positions)
    q2 = partial_rope(q[..., 1, :], positions)
    k1 = partial_rope(k[..., 0, :], positions)
    k2 = partial_rope(k[..., 1, :], positions)
    v = v.reshape(b, s, DIFF_HEADS, DIFF_V_DIM)
    lam_init = 0.8 - 0.6 * math.exp(-0.3 * layer_idx)
    lam = (jnp.exp(jnp.sum(lq1.astype(jnp.float32) * lk1.astype(jnp.float32)))
           - jnp.exp(jnp.sum(lq2.astype(jnp.float32) * lk2.astype(jnp.float32))) + lam_init)
    o = differential_attention(q1, q2, k1, k2, v, lam)
    o = rms_norm(o.reshape(b, s, DIFF_HEADS, DIFF_V_DIM), sub_g) * (1.0 - lam_init)
    return jnp.einsum('bse,ed->bsd', o.reshape(b, s, DIFF_HEADS * DIFF_V_DIM), w_out)


def swiglu_ffn(h, w_gate, w_up, w_down):
    g = jnp.einsum('bsd,df->bsf', h, w_gate)
    u = jnp.einsum('bsd,df->bsf', h, w_up)
    return jnp.einsum('bsf,fd->bsd', jax.nn.silu(g) * u, w_down)


def setup_inputs(seed: int = 0) -> dict:
    key = jax.random.key(seed)
    ks = jax.random.split(key, 16)
    f32 = jnp.float32
    d_attn = 3 * D_MODEL

    def normal(k, shape, scale):
        return jax.random.normal(k, shape, dtype=f32) * scale

    def gain(k, shape):
        return 1.0 + 0.02 * jax.random.normal(k, shape, dtype=f32)

    x = jax.random.normal(ks[0], (BATCH, SEQ, D_MODEL), dtype=f32)
    start = jax.random.randint(ks[1], (BATCH, 1), 0, 4096, dtype=jnp.int32)
    positions = start + jnp.arange(SEQ, dtype=jnp.int32)[None, :]
    return {
        'x': x,
        'positions': positions,
        'attn_norm': gain(ks[2], (DEPTH, D_MODEL)),
        'w_in': normal(ks[3], (DEPTH, D_MODEL, d_attn), D_MODEL ** -0.5),
        'w_out': normal(ks[4], (DEPTH, D_MODEL, D_MODEL), D_MODEL ** -0.5),
        'q_norm': gain(ks[5], (N_DIFF, DIFF_HEAD_DIM)),
        'k_norm': gain(ks[6], (N_DIFF, DIFF_HEAD_DIM)),
        'lambda_q1': normal(ks[7], (N_DIFF, DIFF_HEAD_DIM), 0.1),
        'lambda_k1': normal(ks[8], (N_DIFF, DIFF_HEAD_DIM), 0.1),
        'lambda_q2': normal(ks[9], (N_DIFF, DIFF_HEAD_DIM), 0.1),
        'lambda_k2': normal(ks[10], (N_DIFF, DIFF_HEAD_DIM), 0.1),
        'sub_norm': gain(ks[11], (N_DIFF, DIFF_V_DIM)),
        'ffn_norm': gain(ks[12], (DEPTH, D_MODEL)),
        'w_gate': normal(ks[13], (DEPTH, D_MODEL, D_FF), D_MODEL ** -0.5),
        'w_up': normal(ks[14], (DEPTH, D_MODEL, D_FF), D_MODEL ** -0.5),
        'w_down': normal(ks[15], (DEPTH, D_FF, D_MODEL), D_FF ** -0.5),
    }


def reference(x, positions, attn_norm, w_in, w_out, q_norm, k_norm, lambda_q1, lambda_k1,
              lambda_q2, lambda_k2, sub_norm, ffn_norm, w_gate, w_up, w_down):
    for i in range(DEPTH):
        h = rms_norm(x, attn_norm[i])
        if i % N_MIXERS == 0:
            mix = stick_breaking_mixer(h, w_in[i], w_out[i])
        else:
            j = i // N_MIXERS
            mix = diff_mixer(h, positions, w_in[i], w_out[i], q_norm[j], k_norm[j],
                             lambda_q1[j], lambda_k1[j], lambda_q2[j], lambda_k2[j],
                             sub_norm[j], i)
        x = x + mix
        x = x + swiglu_ffn(rms_norm(x, ffn_norm[i]), w_gate[i], w_up[i], w_down[i])
    return x
```

```python
import math
import os
import re
from contextlib import ExitStack

import numpy as np
import concourse.bass as bass
import concourse.mybir as mybir
from concourse.bass_utils import run_bass_kernel_spmd

F32 = mybir.dt.float32
BF16 = mybir.dt.bfloat16
I32 = mybir.dt.int32
AF = mybir.ActivationFunctionType
ALU = mybir.AluOpType
AX = mybir.AxisListType

D = 1024
DFF = 2816
NFC = DFF // 128
NL = 4
EPS = 1e-6
NEG = -30000.0
ROPE_THETA = 500000.0
ENGS = ("sync", "scalar", "vector", "gpsimd", "tensor")
SAME_ENG_SYNC = True
DIFFDBG = int(os.environ.get('DIFFDBG', '0'))


class Sem:
    def __init__(self, h, name):
        self.h = h
        self.v = 0
        self.name = name


class Ev:
    __slots__ = ("sem", "val")

    def __init__(self, sem=None, val=None):
        self.sem = sem
        self.val = val


class Prog:
    def __init__(self, nc, stack):
        self.nc = nc
        self.stack = stack
        self.sems = {}
        self.nst = 0
        self.store_keys = []
        self.esem = {e: self.new_sem("ev_" + e) for e in ENGS}
        self.q = {e: [] for e in ENGS}
        self.lastw = {}
        self.readers = {}
        self.pending = {e: [] for e in ENGS}
        self.waited = {e: {} for e in ENGS}
        self.nops = 0

    def new_sem(self, name):
        name = re.sub(r"\d+_", "_", name, count=1) if name.count("_") >= 2 else name
        if name in self.sems:
            return self.sems[name]
        h = self.stack.enter_context(self.nc.semaphore(name))
        s = Sem(h, name)
        self.sems[name] = s
        return s

    def store(self, eng, fn, r, dsem):
        key = f"__st{self.nst}"
        self.nst += 1
        self.op(eng, fn, r=r, w=[key], dsem=dsem)
        self.store_keys.append(key)

    def drain(self, scratch_ap):
        keys = list(self.store_keys)
        self.store_keys = []
        self.op("gpsimd", lambda e: e.memset(scratch_ap, 0.0), r=keys, w=["__drain"])

    def op(self, eng, fn, r=(), w=(), ev=True, dsem=None):
        deps = []
        for b in r:
            e = self.lastw.get(b)
            if e is not None:
                deps.append(e)
        for b in w:
            e = self.lastw.get(b)
            if e is not None:
                deps.append(e)
            deps.extend(self.readers.get(b, ()))
        pend = self.pending[eng]
        if pend:
            deps = [d for d in deps if not (d.sem is None and any(d is p for p in pend))]
        n = 0
        if dsem is not None:
            dsem.v += 16
            event = Ev(dsem, dsem.v)
            n = 16
        elif ev:
            s = self.esem[eng]
            s.v += 1
            event = Ev(s, s.v)
            n = 1
            for p in self.pending[eng]:
                p.sem = s
                p.val = s.v
            self.pending[eng] = []
        else:
            event = Ev()
            self.pending[eng].append(event)
        self.q[eng].append((fn, deps, event, n))
        for b in w:
            self.lastw[b] = event
            self.readers[b] = []
        for b in r:
            self.readers.setdefault(b, []).append(event)
        self.nops += 1

    def flush(self):
        nc = self.nc
        for e in ENGS:
            assert not self.pending[e], f"pending events on {e}"
        with nc.Block() as block:
            for eng in ENGS:
                ops = self.q[eng]
                if not ops:
                    continue

                def body(e, ops=ops, eng=eng):
                    waited = self.waited[eng]
                    own = self.esem[eng]
                    for fn, deps, event, n in ops:
                        for d in deps:
                            s, v = d.sem, d.val
                            assert s is not None
                            if s is own and not SAME_ENG_SYNC:
                                continue
                            if waited.get(s, 0) >= v:
                                continue
                            e.wait_ge(s.h, v)
                            waited[s] = v
                        ins = fn(e)
                        if n:
                            ins.then_inc(event.sem.h, n)

                getattr(block, eng)(body)
        self.q = {e: [] for e in ENGS}


def build_nc(S, layers=NL, debug=False, nph=99):
    nt = S // 128
    nc = bass.Bass("TRN2", target_bir_lowering=False)
    okind = "ExternalOutput" if debug else "Internal"

    def din(name, shape, dt=F32):
        return nc.dram_tensor(name, list(shape), dt, kind="ExternalInput").ap()

    x_in = din("x", [S, D])
    pos_in = din("positions", [S], I32)
    attn_norm = din("attn_norm", [NL, D])
    w_in = din("w_in", [NL, D, 3 * D])
    w_out = din("w_out", [NL, D, D])
    q_norm = din("q_norm", [2, 64])
    k_norm = din("k_norm", [2, 64])
    lq1 = din("lambda_q1", [2, 64])
    lk1 = din("lambda_k1", [2, 64])
    lq2 = din("lambda_q2", [2, 64])
    lk2 = din("lambda_k2", [2, 64])
    sub_norm = din("sub_norm", [2, 128])
    ffn_norm = din("ffn_norm", [NL, D])
    w_gate = din("w_gate", [NL, D, DFF])
    w_up = din("w_up", [NL, D, DFF])
    w_down = din("w_down", [NL, DFF, D])
    out = nc.dram_tensor("out", [S, D], F32, kind="ExternalOutput").ap()
    xs = [nc.dram_tensor(f"xs{i}", [S, D], F32, kind=okind).ap() for i in range(2)]
    qT_d = nc.dram_tensor("qT_d", [D, S], BF16, kind=okind).ap()
    kT_d = nc.dram_tensor("kT_d", [D, S], BF16, kind=okind).ap()
    oT_d = nc.dram_tensor("oT_d", [D, S], BF16, kind=okind).ap()

    with ExitStack() as gs, nc.allow_low_precision("bf16 matmul operands, fp32 accumulation"):
        P = Prog(nc, gs)

        uid = [0]

        def sb(stack, name, shape, dt):
            uid[0] += 1
            return stack.enter_context(nc.sbuf_tensor(f"{name}_u{uid[0]}", list(shape), dt))

        def ps(stack, name, shape, dt):
            uid[0] += 1
            return stack.enter_context(nc.psum_tensor(f"{name}_u{uid[0]}", list(shape), dt))

        ident = sb(gs, "ident", [128, 128], BF16)
        negones = sb(gs, "negones", [128, 128], BF16)
        ntri = sb(gs, "ntri", [128, 128], BF16)
        maskneg = sb(gs, "maskneg", [128, 128], BF16)
        mask2 = sb(gs, "mask2", [128, 128], BF16)
        g_attn = sb(gs, "g_attn", [128, NL, 8], F32)
        g_ffn = sb(gs, "g_ffn", [128, NL, 8], F32)
        cosT = sb(gs, "cosT", [128, nt, 8], F32)
        sinT = sb(gs, "sinT", [128, nt, 8], F32)
        gq = sb(gs, "gq", [128, 2, 64], F32)
        gk = sb(gs, "gk", [128, 2, 64], F32)
        neglam = sb(gs, "neglam", [128, 2], F32)
        subg = sb(gs, "subg", [128, 2], F32)
        mpi = sb(gs, "mpi", [128, 1], F32)
        csem = P.new_sem("csem")
        ts = ExitStack()
        onesb = sb(ts, "onesb", [128, 128], BF16)
        negbig = sb(ts, "negbig", [128, 128], BF16)
        posi = sb(ts, "posi", [128, nt], I32)
        posf = sb(ts, "posf", [128, nt], F32)
        ang = sb(ts, "ang", [128, nt, 8], F32)
        ang2 = sb(ts, "ang2", [128, nt, 8], F32)
        lam_in = sb(ts, "lam_in", [128, 8, 64], F32)
        lam_tmp = sb(ts, "lam_tmp", [128, 4, 64], F32)
        lam_s = sb(ts, "lam_s", [128, 4], F32)
        lam_e = sb(ts, "lam_e", [128, 4], F32)

        V, G, A, T, SY = "vector", "gpsimd", "scalar", "tensor", "sync"

        P.op(G, lambda e: e.memset(onesb[:], 1.0), w=["onesb"])
        P.op(G, lambda e: e.memset(negones[:], -1.0), w=["negones"])
        P.op(G, lambda e: e.memset(negbig[:], NEG), w=["negbig"])
        P.op(G, lambda e: e.memset(mpi[:], -math.pi), w=["mpi"])
        P.op(G, lambda e: e.affine_select(out=ident[:], in_=onesb[:], pattern=[[1, 128]],
                                          compare_op=ALU.is_equal, fill=0.0, base=0,
                                          channel_multiplier=-1), r=["onesb"], w=["ident"])
        P.op(G, lambda e: e.affine_select(out=ntri[:], in_=negones[:], pattern=[[-1, 128]],
                                          compare_op=ALU.is_ge, fill=0.0, base=0,
                                          channel_multiplier=1), r=["negones"], w=["ntri"])
        P.op(G, lambda e: e.affine_select(out=maskneg[:], in_=negbig[:], pattern=[[-1, 128]],
                                          compare_op=ALU.is_ge, fill=0.0, base=0,
                                          channel_multiplier=1), r=["negbig"], w=["maskneg"])
        P.op(G, lambda e: e.memset(mask2[:], 0.0), w=["mask2"])
        P.op(G, lambda e: e.memset(mask2[64:128, 0:64], NEG), w=["mask2"])

        def small_dma(dst, src, key):
            P.op(SY, lambda e: e.dma_start(out=dst, in_=src), w=["consts"], dsem=csem)

        identF = sb(ts, "identF", [128, 128], F32)
        P.op(V, lambda e: e.tensor_copy(out=identF[:], in_=ident[:]), r=["ident"], w=["identF"])
        rows = sb(ts, "c_rows", [2 * NL * 8 + nt + 2, 128], F32)
        posr = sb(ts, "c_posr", [nt, 128], I32)
        NR = 2 * NL * 8
        small_dma(rows[0:NL * 8, :], attn_norm.rearrange("l (c p) -> (l c) p", p=128), "rows")
        small_dma(rows[NL * 8:NR, :], ffn_norm.rearrange("l (c p) -> (l c) p", p=128), "rows")
        small_dma(posr[:], pos_in.rearrange("(t p) -> t p", p=128), "posr")
        for j in range(2):
            small_dma(gq[:, j, :], q_norm[j, :].partition_broadcast(128), "gq")
            small_dma(gk[:, j, :], k_norm[j, :].partition_broadcast(128), "gk")
            for i, lt in enumerate((lq1, lk1, lq2, lk2)):
                small_dma(lam_in[:, j * 4 + i, :], lt[j, :].partition_broadcast(128), "lam_in")
        subr = sb(ts, "c_subr", [2, 128], F32)
        small_dma(subr[:], sub_norm, "subr")
        posrf = sb(ts, "c_posrf", [nt, 128], F32)
        P.op(V, lambda e: e.tensor_copy(out=posrf[:], in_=posr[:]), r=["consts"], w=["posrf"])
        with ExitStack() as cst:
            pc = ps(cst, "pc", [128, 512], F32)
            P.op(T, lambda e: e.transpose(out=pc[:, 0:NR], in_=rows[0:NR, :], identity=identF[0:NR, 0:NR]),
                 r=["consts", "identF"], w=["pc"])
            P.op(T, lambda e: e.transpose(out=pc[:, 64:64 + nt], in_=posrf[:], identity=identF[0:nt, 0:nt]),
                 r=["posrf", "identF"], w=["pc"])
            P.op(T, lambda e: e.transpose(out=pc[:, 128:130], in_=subr[:], identity=identF[0:2, 0:2]),
                 r=["consts", "identF"], w=["pc"])
            P.op(V, lambda e: e.tensor_copy(out=g_attn[:].rearrange("p l c -> p (l c)"), in_=pc[:, 0:NL * 8]), r=["pc"], w=["g_attn"])
            P.op(V, lambda e: e.tensor_copy(out=g_ffn[:].rearrange("p l c -> p (l c)"), in_=pc[:, NL * 8:NR]), r=["pc"], w=["g_ffn"])
            P.op(V, lambda e: e.tensor_copy(out=posf[:], in_=pc[:, 64:64 + nt]), r=["pc"], w=["posf"])
            P.op(V, lambda e: e.tensor_copy(out=subg[:], in_=pc[:, 128:130]), r=["pc"], w=["subg"])
            P.flush()

        inv_freq = (np.float32(ROPE_THETA) ** (-np.arange(0, 16, 2, dtype=np.float32) / np.float32(16))).astype(np.float32)
        for i in range(8):
            P.op(V, lambda e, i=i: e.tensor_scalar(out=ang[:, :, i], in0=posf[:], scalar1=float(inv_freq[i]),
                                                   scalar2=None, op0=ALU.mult), r=["posf"], w=["ang"], ev=(i == 7))
        twopi = 2.0 * math.pi
        C1 = 6.28125
        C2 = twopi - C1
        ki = sb(ts, "rope_ki", [128, nt, 8], I32)
        kf = sb(ts, "rope_kf", [128, nt, 8], F32)
        mk = sb(ts, "rope_mk", [128, nt, 8], F32)

        def vop(fn, r, w):
            P.op(V, fn, r=r, w=w)

        vop(lambda e: e.tensor_scalar(out=kf[:], in0=ang[:], scalar1=1.0 / twopi, scalar2=None, op0=ALU.mult), ["ang"], ["kf"])
        vop(lambda e: e.tensor_copy(out=ki[:], in_=kf[:]), ["kf"], ["ki"])
        vop(lambda e: e.tensor_copy(out=kf[:], in_=ki[:]), ["ki"], ["kf"])
        vop(lambda e: e.scalar_tensor_tensor(out=ang2[:], in0=kf[:], scalar=-C1, in1=ang[:], op0=ALU.mult, op1=ALU.add),
            ["kf", "ang"], ["ang2"])
        vop(lambda e: e.scalar_tensor_tensor(out=ang2[:], in0=kf[:], scalar=-C2, in1=ang2[:], op0=ALU.mult, op1=ALU.add),
            ["kf", "ang2"], ["ang2"])

        def wrap(buf, key):
            vop(lambda e: e.tensor_scalar(out=mk[:], in0=buf[:], scalar1=math.pi, scalar2=None, op0=ALU.is_gt), [key], ["mk"])
            vop(lambda e: e.scalar_tensor_tensor(out=buf[:], in0=mk[:], scalar=-twopi, in1=buf[:], op0=ALU.mult, op1=ALU.add),
                ["mk", key], [key])
            vop(lambda e: e.tensor_scalar(out=mk[:], in0=buf[:], scalar1=-math.pi, scalar2=None, op0=ALU.is_lt), [key], ["mk"])
            vop(lambda e: e.scalar_tensor_tensor(out=buf[:], in0=mk[:], scalar=twopi, in1=buf[:], op0=ALU.mult, op1=ALU.add),
                ["mk", key], [key])
            vop(lambda e: e.tensor_scalar(out=buf[:], in0=buf[:], scalar1=math.pi, scalar2=-math.pi, op0=ALU.min, op1=ALU.max),
                [key], [key])

        wrap(ang2, "ang2")
        P.op(A, lambda e: e.activation(out=sinT[:], in_=ang2[:], func=AF.Sin), r=["ang2"], w=["sinT"])
        vop(lambda e: e.tensor_scalar(out=ang[:], in0=ang2[:], scalar1=math.pi / 2, scalar2=None, op0=ALU.add), ["ang2", "ang"], ["ang"])
        wrap(ang, "ang")
        P.op(A, lambda e: e.activation(out=cosT[:], in_=ang[:], func=AF.Sin), r=["ang"], w=["cosT"])
        for j in range(2):
            for i in range(2):
                k = j * 2 + i
                P.op(V, lambda e, j=j, i=i, k=k: e.tensor_tensor(out=lam_tmp[:, k, :], in0=lam_in[:, j * 4 + 2 * i, :],
                                                               in1=lam_in[:, j * 4 + 2 * i + 1, :], op=ALU.mult),
                     r=["consts"], w=["lam_tmp"])
        P.op(V, lambda e: e.tensor_reduce(out=lam_s[:], in_=lam_tmp[:], axis=AX.X, op=ALU.add),
             r=["lam_tmp"], w=["lam_s"])
        P.op(A, lambda e: e.activation(out=lam_e[:], in_=lam_s[:], func=AF.Exp), r=["lam_s"], w=["lam_e"])
        for j in range(2):
            lam_init = 0.8 - 0.6 * math.exp(-0.3 * (2 * j + 1))
            P.op(V, lambda e, j=j: e.tensor_tensor(out=neglam[:, j:j + 1], in0=lam_e[:, 2 * j + 1:2 * j + 2],
                                                   in1=lam_e[:, 2 * j:2 * j + 1], op=ALU.subtract),
                 r=["lam_e"], w=["neglam"])
            P.op(V, lambda e, j=j, li=lam_init: e.tensor_scalar(out=neglam[:, j:j + 1], in0=neglam[:, j:j + 1],
                                                                scalar1=-li, scalar2=None, op0=ALU.add),
                 r=["neglam"], w=["neglam"])
            P.op(V, lambda e, j=j, li=lam_init: e.tensor_scalar(out=subg[:, j:j + 1], in0=subg[:, j:j + 1],
                                                                scalar1=(1.0 - li), scalar2=None, op0=ALU.mult),
                 r=["subg"], w=["subg"])
        if debug:
            dcos = nc.dram_tensor("dbg_cos", [128, nt * 8], F32, kind="ExternalOutput").ap()
            dsin = nc.dram_tensor("dbg_sin", [128, nt * 8], F32, kind="ExternalOutput").ap()
            dmisc = nc.dram_tensor("dbg_misc", [128, 4 + 2 * NL * 8], F32, kind="ExternalOutput").ap()
            P.store(SY, lambda e: e.dma_start(out=dcos, in_=cosT[:].rearrange("p t i -> p (t i)")), r=["cosT"], dsem=csem)
            P.store(SY, lambda e: e.dma_start(out=dsin, in_=sinT[:].rearrange("p t i -> p (t i)")), r=["sinT"], dsem=csem)
            P.store(SY, lambda e: e.dma_start(out=dmisc[:, 0:2], in_=neglam[:]), r=["neglam"], dsem=csem)
            P.store(SY, lambda e: e.dma_start(out=dmisc[:, 2:4], in_=subg[:]), r=["subg"], dsem=csem)
            P.store(SY, lambda e: e.dma_start(out=dmisc[:, 4:4 + NL * 8], in_=g_attn[:].rearrange("p l c -> p (l c)")), r=["g_attn"], dsem=csem)
            P.store(SY, lambda e: e.dma_start(out=dmisc[:, 4 + NL * 8:4 + 2 * NL * 8], in_=g_ffn[:].rearrange("p l c -> p (l c)")), r=["g_ffn"], dsem=csem)
            P.drain(mpi[:, 0:1])
        P.flush()
        ts.close()

        x_cur = x_in
        for l in range(layers):
            diff = (l % 2 == 1)
            jd = l // 2
            x_next = out if l == layers - 1 else xs[l % 2]
            with ExitStack() as ls:
                vw = 8 * 130 if diff else D
                v_sb = sb(ls, f"v_sb{l}", [128, nt, vw], BF16)
                if diff:
                    for h in range(8):
                        P.op(G, lambda e, h=h: e.memset(v_sb[:, :, h * 130 + 128:h * 130 + 130], 1.0), w=["v_sb"])
                if nph >= 1:
                    phase_qkv(nc, P, sb, ps, l, S, diff, jd, x_cur, w_in, g_attn, v_sb, qT_d, kT_d, ident,
                              gq, gk, cosT, sinT)
                if nph < 2:
                    pass
                elif diff:
                    phase_attn_diff(nc, P, sb, ps, l, S, jd, v_sb, qT_d, kT_d, oT_d, ident, mask2, neglam)
                else:
                    phase_attn_sb(nc, P, sb, ps, l, S, v_sb, qT_d, kT_d, oT_d, ident, ntri, negones, maskneg)
            if nph >= 3:
                phase_ffn(nc, P, sb, ps, l, S, diff, jd, x_cur, x_next, oT_d, w_out, w_gate, w_up, w_down,
                          g_ffn, subg, ident)
            x_cur = x_next
    return nc


def load_weight(P, sb_w, stage, ssem, dram_rows, nchunk, width, scale_ap_fn, key, step=1024, extra_r=("g_ffn", "subg")):
    nst = len(stage)
    k = 0
    for c in range(nchunk):
        for o in range(0, width, step):
            wdt = min(step, width - o)
            s = k % nst
            k += 1
            P.op("sync" if s % 2 == 0 else "gpsimd", lambda e, c=c, o=o, wdt=wdt, s=s: e.dma_start(out=stage[s][:, 0:wdt],
                                                                      in_=dram_rows(c)[:, o:o + wdt]),
                 w=[f"wst{s}"], dsem=ssem[s])
            sc = scale_ap_fn(c) if scale_ap_fn is not None else None
            eng = "vector" if (k % 2 == 0) else "scalar"
            dst = sb_w[:, c, o:o + wdt]
            srcap = stage[s][:, 0:wdt]
            rr = [f"wst{s}"] + (list(extra_r) if sc is not None else [])
            if eng == "vector":
                if sc is None:
                    P.op(eng, lambda e, dst=dst, srcap=srcap: e.tensor_copy(out=dst, in_=srcap), r=rr, w=[key])
                else:
                    P.op(eng, lambda e, dst=dst, srcap=srcap, sc=sc: e.tensor_scalar(out=dst, in0=srcap, scalar1=sc, scalar2=None,
                                                                                   op0=ALU.mult), r=rr, w=[key])
            else:
                if sc is None:
                    P.op(eng, lambda e, dst=dst, srcap=srcap: e.copy(out=dst, in_=srcap), r=rr, w=[key])
                else:
                    P.op(eng, lambda e, dst=dst, srcap=srcap, sc=sc: e.activation(out=dst, in_=srcap, func=AF.Copy, scale=sc),
                         r=rr, w=[key])


def rmsnorm_tile(P, xt_ap, xkey, junk, ss, rstd, hb, tag, n=D):
    P.op("scalar", lambda e: e.activation(out=junk, in_=xt_ap, func=AF.Square, accum_out=ss),
         r=[xkey], w=["junk" + tag, "ss" + tag])
    P.op("vector", lambda e: e.tensor_scalar(out=rstd, in0=ss, scalar1=1.0 / n, scalar2=EPS,
                                             op0=ALU.mult, op1=ALU.add), r=["ss" + tag], w=["rstd" + tag])
    P.op("scalar", lambda e: e.activation(out=rstd, in_=rstd, func=AF.Ln), r=["rstd" + tag], w=["rstd" + tag])
    P.op("scalar", lambda e: e.activation(out=rstd, in_=rstd, func=AF.Exp, scale=-0.5), r=["rstd" + tag], w=["rstd" + tag])
    P.op("vector", lambda e: e.tensor_scalar(out=hb, in0=xt_ap, scalar1=rstd, scalar2=None, op0=ALU.mult),
         r=[xkey, "rstd" + tag, "junk" + tag], w=["hb" + tag])


def phase_qkv(nc, P, sb, ps, l, S, diff, jd, x_cur, w_in, g_attn, v_sb, qT_d, kT_d, ident, gq, gk, cosT, sinT):
    nt = S // 128
    with ExitStack() as st:
        wq = sb(st, "wq", [128, 8, 3 * D], BF16)
        stage = [sb(st, f"wst{i}", [128, 1024], F32) for i in range(2)]
        ssem = [P.new_sem(f"qkv_ws{l}_{i}") for i in range(2)]
        xt = [sb(st, f"xt{i}", [128, D], F32) for i in range(2)]
        xsem = [P.new_sem(f"qkv_x{l}_{i}") for i in range(2)]
        junk = sb(st, "junkq", [128, D], BF16)
        ss = sb(st, "ssq", [128, 2], F32)
        rstd = sb(st, "rstdq", [128, 2], F32)
        hb = [sb(st, f"hbq{i}", [128, D], BF16) for i in range(2)]
        hTs = [sb(st, f"hT{i}", [128, 8, 128], BF16) for i in range(2)]
        qk_toks = [sb(st, f"qk_tok{i}", [128, 2 * D], BF16) for i in range(2)]
        qkT = [sb(st, f"qkT{i}", [128, 16, 512], BF16) for i in range(2)]
        osem = [P.new_sem(f"qkv_o{l}_{i}") for i in range(2)]
        if diff:
            qkf = sb(st, "qkf", [128, 2 * D], F32)
            sq2 = sb(st, "sq2", [128, 2 * D], F32)
            ssq = sb(st, "ssq2", [128, 32], F32)
            rs = sb(st, "rs", [128, 32], F32)
            rt = [sb(st, f"rt{i}", [128, 32, 8], F32) for i in range(4)]
        pqkv = [ps(st, f"pqkv{i}", [128, 512], F32) for i in range(6)]
        pT = [ps(st, f"pT{i}", [128, 8, 128], BF16) for i in range(2)]

        load_weight(P, wq, stage, ssem, lambda c: w_in[l, c * 128:(c + 1) * 128, :], 8, 3 * D,
                    lambda c: g_attn[:, l, c:c + 1], "wq", extra_r=["g_attn"])

        def front_a(tt):
            s = tt % 2
            hT = hTs[tt % 2]
            HT = f"hT{tt % 2}"
            P.op("sync", lambda e, tt=tt, s=s: e.dma_start(out=xt[s][:], in_=x_cur[tt * 128:(tt + 1) * 128, :]),
                 w=[f"xt{s}"], dsem=xsem[s])
            tag = f"q{s}"
            rmsnorm_tile(P, xt[s][:], f"xt{s}", junk[:], ss[:, s:s + 1], rstd[:, s:s + 1], hb[s][:], tag)
            for c in range(8):
                P.op("tensor", lambda e, c=c, s=s: e.transpose(out=pT[0][:, c, :], in_=hb[s][:, c * 128:(c + 1) * 128],
                                                              identity=ident[:]),
                     r=["hb" + tag, "ident"], w=["pT0"], ev=(c == 7))
            P.op("scalar", lambda e: e.copy(out=hT[:], in_=pT[0][:]), r=["pT0"], w=[HT])

        def front_b(tt):
            s = tt % 2
            j4 = tt % 4
            gi = tt // 4
            qs = gi % 2
            qk_tok = qk_toks[tt % 2]
            QK = f"qk_tok{tt % 2}"
            hT = hTs[tt % 2]
            HT = f"hT{tt % 2}"
            for r_ in range(6):
                for c in range(8):
                    P.op("tensor", lambda e, c=c, r_=r_: e.matmul(pqkv[r_][:], lhsT=hT[:, c, :],
                                                                  rhs=wq[:, c, r_ * 512:(r_ + 1) * 512],
                                                                  start=(c == 0), stop=(c == 7)),
                         r=[HT, "wq"], w=[f"pqkv{r_}"], ev=(c == 7))
            if not diff:
                for r_ in range(2):
                    P.op("scalar", lambda e, r_=r_: e.mul(out=qk_tok[:, r_ * 512:(r_ + 1) * 512], in_=pqkv[r_][:], mul=0.125),
                         r=[f"pqkv{r_}"], w=[QK])
                for r_ in range(2, 4):
                    P.op("vector", lambda e, r_=r_: e.tensor_copy(out=qk_tok[:, r_ * 512:(r_ + 1) * 512], in_=pqkv[r_][:]),
                         r=[f"pqkv{r_}"], w=[QK])
                P.op("scalar", lambda e, tt=tt: e.copy(out=v_sb[:, tt, 0:512], in_=pqkv[4][:]), r=["pqkv4"], w=["v_sb"])
                P.op("vector", lambda e, tt=tt: e.tensor_copy(out=v_sb[:, tt, 512:1024], in_=pqkv[5][:]),
                     r=["pqkv5"], w=["v_sb"])
            else:
                for r_ in range(4):
                    P.op("scalar", lambda e, r_=r_: e.activation(out=sq2[:, r_ * 512:(r_ + 1) * 512], in_=pqkv[r_][:],
                                                                 func=AF.Square), r=[f"pqkv{r_}"], w=["sq2"])
                P.op("vector", lambda e: e.tensor_reduce(out=ssq[:], in_=sq2[:].rearrange("p (g d) -> p g d", d=64),
                                                         axis=AX.X, op=ALU.add), r=["sq2"], w=["ssq"])
                P.op("vector", lambda e: e.tensor_scalar(out=rs[:], in0=ssq[:], scalar1=1.0 / 64, scalar2=EPS,
                                                         op0=ALU.mult, op1=ALU.add), r=["ssq"], w=["rs"])
                P.op("scalar", lambda e: e.activation(out=rs[:], in_=rs[:], func=AF.Ln), r=["rs"], w=["rs"])
                P.op("scalar", lambda e: e.activation(out=rs[:], in_=rs[:], func=AF.Exp, scale=-0.5), r=["rs"], w=["rs"])
                P.op("vector", lambda e: e.tensor_scalar(out=rs[:, 0:16], in0=rs[:, 0:16], scalar1=0.125, scalar2=None,
                                                         op0=ALU.mult), r=["rs"], w=["rs"])
                for r_ in range(4):
                    P.op("vector", lambda e, r_=r_: e.tensor_tensor(
                        out=qkf[:, r_ * 512:(r_ + 1) * 512].rearrange("p (g d) -> p g d", d=64),
                        in0=pqkv[r_][:].rearrange("p (g d) -> p g d", d=64),
                        in1=rs[:, r_ * 8:(r_ + 1) * 8].unsqueeze(2).to_broadcast([128, 8, 64]), op=ALU.mult),
                         r=[f"pqkv{r_}", "rs"], w=["qkf"])
                for half, gt in ((0, gq), (1, gk)):
                    P.op("vector", lambda e, half=half, gt=gt: e.tensor_tensor(
                        out=qkf[:, half * D:(half + 1) * D].rearrange("p (g d) -> p g d", d=64),
                        in0=qkf[:, half * D:(half + 1) * D].rearrange("p (g d) -> p g d", d=64),
                        in1=gt[:, jd, :].unsqueeze(1).to_broadcast([128, 16, 64]), op=ALU.mult),
                         r=["qkf", "consts"], w=["qkf"])
                qv = qkf[:].rearrange("p (g d) -> p g d", d=64)
                qo = qk_tok[:].rearrange("p (g d) -> p g d", d=64)
                cb = cosT[:, tt, :].unsqueeze(1).to_broadcast([128, 32, 8])
                sbb = sinT[:, tt, :].unsqueeze(1).to_broadcast([128, 32, 8])
                x1 = qv[:, :, 0:8]
                x2 = qv[:, :, 8:16]
                P.op("vector", lambda e, x1=x1, cb=cb: e.tensor_tensor(out=rt[0][:], in0=x1, in1=cb, op=ALU.mult),
                     r=["qkf", "cosT"], w=["rt0"])
                P.op("vector", lambda e, x2=x2, sbb=sbb: e.tensor_tensor(out=rt[1][:], in0=x2, in1=sbb, op=ALU.mult),
                     r=["qkf", "sinT"], w=["rt1"])
                P.op("vector", lambda e, x2=x2, cb=cb: e.tensor_tensor(out=rt[2][:], in0=x2, in1=cb, op=ALU.mult),
                     r=["qkf", "cosT"], w=["rt2"])
                P.op("vector", lambda e, x1=x1, sbb=sbb: e.tensor_tensor(out=rt[3][:], in0=x1, in1=sbb, op=ALU.mult),
                     r=["qkf", "sinT"], w=["rt3"])
                P.op("vector", lambda e, qo=qo: e.tensor_tensor(out=qo[:, :, 0:8], in0=rt[0][:], in1=rt[1][:], op=ALU.subtract),
                     r=["rt0", "rt1"], w=[QK])
                P.op("vector", lambda e, qo=qo: e.tensor_tensor(out=qo[:, :, 8:16], in0=rt[2][:], in1=rt[3][:], op=ALU.add),
                     r=["rt2", "rt3"], w=[QK])
                P.op("vector", lambda e, qo=qo, qv=qv: e.tensor_copy(out=qo[:, :, 16:64], in_=qv[:, :, 16:64]),
                     r=["qkf"], w=[QK])
                vv = v_sb[:, tt, :].rearrange("p (h d) -> p h d", d=130)
                P.op("scalar", lambda e, vv=vv: e.copy(out=vv[:, 0:4, 0:128], in_=pqkv[4][:].rearrange("p (h d) -> p h d", d=128)),
                     r=["pqkv4"], w=["v_sb"])
                P.op("vector", lambda e, vv=vv: e.tensor_copy(out=vv[:, 4:8, 0:128], in_=pqkv[5][:].rearrange("p (h d) -> p h d", d=128)),
                     r=["pqkv5"], w=["v_sb"])
        def back(tt):
            s = tt % 2
            j4 = tt % 4
            gi = tt // 4
            qs = gi % 2
            qk_tok = qk_toks[tt % 2]
            QK = f"qk_tok{tt % 2}"
            for half in range(2):
                for c in range(8):
                    cc = half * 8 + c
                    P.op("tensor", lambda e, cc=cc, c=c: e.transpose(out=pT[1][:, c, :], in_=qk_tok[:, cc * 128:(cc + 1) * 128],
                                                                    identity=ident[:]),
                         r=[QK, "ident"], w=["pT1"], ev=(c == 7))
                eng = "scalar" if half == 0 else "vector"
                if eng == "scalar":
                    P.op(eng, lambda e, half=half, j4=j4, qs=qs: e.copy(out=qkT[qs][:, half * 8:(half + 1) * 8, j4 * 128:(j4 + 1) * 128],
                                                                        in_=pT[1][:]), r=["pT1"], w=[f"qkT{qs}"])
                else:
                    P.op(eng, lambda e, half=half, j4=j4, qs=qs: e.tensor_copy(out=qkT[qs][:, half * 8:(half + 1) * 8, j4 * 128:(j4 + 1) * 128],
                                                                               in_=pT[1][:]), r=["pT1"], w=[f"qkT{qs}"])
            if j4 == 3 or tt == nt - 1:
                t0 = gi * 512
                wdt = (j4 + 1) * 128
                P.store("gpsimd", lambda e, t0=t0, wdt=wdt, qs=qs: e.dma_start(
                    out=qT_d[:, t0:t0 + wdt].rearrange("(c p) t -> p c t", p=128), in_=qkT[qs][:, 0:8, 0:wdt]),
                        r=[f"qkT{qs}"], dsem=osem[qs])
                P.store("gpsimd", lambda e, t0=t0, wdt=wdt, qs=qs: e.dma_start(
                    out=kT_d[:, t0:t0 + wdt].rearrange("(c p) t -> p c t", p=128), in_=qkT[qs][:, 8:16, 0:wdt]),
                        r=[f"qkT{qs}"], dsem=osem[qs])
        for tt in range(nt + 1):
            if tt < nt:
                front_a(tt)
            if tt >= 1:
                back(tt - 1)
            if tt < nt:
                front_b(tt)
        P.drain(junk[:, 0:1])
        P.flush()


def phase_attn_sb(nc, P, sb, ps, l, S, v_sb, qT_d, kT_d, oT_d, ident, ntri, negones, maskneg):
    ng = S // 512
    with ExitStack() as st:
        qTp = [sb(st, f"qTp{i}", [128, S], BF16) for i in range(2)]
        kTp = [sb(st, f"kTp{i}", [128, S], BF16) for i in range(2)]
        lsem = [P.new_sem(f"sb_ld{l}_{i}") for i in range(2)]
        Ebs = [sb(st, f"Eb{i}", [128, 512], F32) for i in range(2)]
        SPb = [sb(st, f"SPb{i}", [128, 512], BF16) for i in range(3)]
        R32 = [sb(st, f"R32_{i}", [128, 512], F32) for i in range(2)]
        Rb = [sb(st, f"Rb{i}", [128, 512], BF16) for i in range(2)]
        Wb = [sb(st, f"Wb{i}", [128, 512], BF16) for i in range(2)]
        oTs = [sb(st, f"oTs{i}", [64, 512], BF16) for i in range(2)]
        osem = [P.new_sem(f"sb_o{l}_{i}") for i in range(2)]
        Zb = [ps(st, f"Zb{i}", [128, 512], F32) for i in range(2)]
        Lb = [ps(st, f"Lb{i}", [128, 512], F32) for i in range(3)]
        OT = [ps(st, f"OT{i}", [128, 512], F32) for i in range(2)]

        iters = []
        gcount = 0
        for hp in range(8):
            for hh in range(2):
                for g in range(ng):
                    kbs = list(range(4 * g + 3, -1, -1))
                    for idx, kb in enumerate(kbs):
                        iters.append(dict(hp=hp, hh=hh, g=g, kb=kb, first=(idx == 0), last=(kb == 0), gi=gcount,
                                          band=(kb - 4 * g) if kb >= 4 * g else -1))
                    gcount += 1
        for n, it in enumerate(iters):
            it["n"] = n
        loaded = set()

        def ensure_loaded(hp):
            if hp in loaded or hp >= 8:
                return
            loaded.add(hp)
            s = hp % 2
            P.op("sync", lambda e: e.dma_start(out=qTp[s][:], in_=qT_d[hp * 128:(hp + 1) * 128, :]),
                 w=[f"qTp{s}"], dsem=lsem[s])
            P.op("sync", lambda e: e.dma_start(out=kTp[s][:], in_=kT_d[hp * 128:(hp + 1) * 128, :]),
                 w=[f"kTp{s}"], dsem=lsem[s])

        def cols(it):
            c0 = 128 * it["band"] if it["band"] >= 0 else 0
            return c0, 512

        def S1(it):
            n, hp, hh, g, kb = it["n"], it["hp"], it["hh"], it["g"], it["kb"]
            s = hp % 2
            pb = hh * 64
            c0, c1 = cols(it)
            t0 = g * 512
            band = it["band"] >= 0
            kT = kTp[s][pb:pb + 64, kb * 128:(kb + 1) * 128]
            qT = qTp[s][pb:pb + 64, t0 + c0:t0 + c1]
            zb, lb = Zb[n % 2], Lb[n % 3]
            P.op("tensor", lambda e: e.matmul(zb[:, c0:c1], lhsT=kT, rhs=qT, start=True, stop=not band),
                 r=[f"qTp{s}", f"kTp{s}"], w=[f"Zb{n % 2}"], ev=not band)
            if band:
                P.op("tensor", lambda e: e.matmul(zb[:, c0:c0 + 128], lhsT=ident[:], rhs=maskneg[:], start=False, stop=True),
                     r=["ident", "maskneg"], w=[f"Zb{n % 2}"])
            P.op("tensor", lambda e: e.matmul(lb[:, c0:c1], lhsT=kT, rhs=qT, start=True, stop=False),
                 r=[f"qTp{s}", f"kTp{s}"], w=[f"Lb{n % 3}"], ev=False)
            if band:
                P.op("tensor", lambda e: e.matmul(lb[:, c0:c0 + 128], lhsT=ident[:], rhs=maskneg[:], start=False, stop=False),
                     r=["ident", "maskneg"], w=[f"Lb{n % 3}"], ev=False)

        def S2(it):
            n = it["n"]
            c0, c1 = cols(it)
            zb = Zb[n % 2]
            sp = SPb[n % 3]
            eb = Ebs[n % 2]
            P.op("scalar", lambda e: e.activation(out=eb[:, c0:c1], in_=zb[:, c0:c1], func=AF.Exp),
                 r=[f"Zb{n % 2}"], w=[f"Eb{n % 2}"])

        def S2b(it):
            n = it["n"]
            c0, c1 = cols(it)
            sp = SPb[n % 3]
            eb = Ebs[n % 2]
            P.op("scalar", lambda e: e.activation(out=sp[:, c0:c1], in_=eb[:, c0:c1], func=AF.Ln, bias=1.0, scale=1.0),
                 r=[f"Eb{n % 2}"], w=[f"SPb{n % 3}"])
            if it["last"]:
                return
            gi = it["gi"]
            r32 = R32[gi % 2]
            rb = Rb[n % 2]
            if it["band"] >= 0:
                P.op("vector", lambda e: e.tensor_copy(out=r32[:, c0:c0 + 128], in_=sp[:, c0:c0 + 128]),
                     r=[f"SPb{n % 3}"], w=[f"R32_{gi % 2}"])
                if c0 + 128 < 512:
                    P.op("vector", lambda e: e.tensor_tensor(out=r32[:, c0 + 128:512], in0=r32[:, c0 + 128:512],
                                                             in1=sp[:, c0 + 128:512], op=ALU.add),
                         r=[f"SPb{n % 3}", f"R32_{gi % 2}"], w=[f"R32_{gi % 2}"])
            else:
                P.op("vector", lambda e: e.tensor_tensor(out=r32[:], in0=r32[:], in1=sp[:], op=ALU.add),
                     r=[f"SPb{n % 3}", f"R32_{gi % 2}"], w=[f"R32_{gi % 2}"])
            P.op("vector", lambda e: e.tensor_copy(out=rb[:, c0:c1], in_=r32[:, c0:c1]),
                 r=[f"R32_{gi % 2}"], w=[f"Rb{n % 2}"])

        def S4(it):
            n = it["n"]
            c0, c1 = cols(it)
            lb = Lb[n % 3]
            sp = SPb[n % 3]
            if not it["first"]:
                r0 = c0 + 128 if it["band"] >= 0 else 0
                rb = Rb[(n - 1) % 2]
                P.op("tensor", lambda e: e.matmul(lb[:, r0:c1], lhsT=negones[:], rhs=rb[:, r0:c1], start=False, stop=False),
                     r=["negones", f"Rb{(n - 1) % 2}"], w=[f"Lb{n % 3}"], ev=False)
            P.op("tensor", lambda e: e.matmul(lb[:, c0:c1], lhsT=ntri[:], rhs=sp[:, c0:c1], start=False, stop=True),
                 r=["ntri", f"SPb{n % 3}"], w=[f"Lb{n % 3}"])

        def S5(it):
            n = it["n"]
            c0, c1 = cols(it)
            lb = Lb[n % 3]
            wb = Wb[n % 2]
            P.op("scalar", lambda e: e.activation(out=wb[:, c0:c1], in_=lb[:, c0:c1], func=AF.Exp),
                 r=[f"Lb{n % 3}"], w=[f"Wb{n % 2}"])

        def S7(it):
            n, hp, hh, g, kb, gi = it["n"], it["hp"], it["hh"], it["g"], it["kb"], it["gi"]
            h = 2 * hp + hh
            c0, c1 = cols(it)
            wb = Wb[n % 2]
            ot = OT[gi % 2]
            P.op("tensor", lambda e: e.matmul(ot[0:64, c0:c1], lhsT=v_sb[:, kb, h * 64:(h + 1) * 64], rhs=wb[:, c0:c1],
                                              start=it["first"], stop=it["last"], skip_group_check=True),
                 r=["v_sb", f"Wb{n % 2}"], w=[f"OT{gi % 2}"])
            if it["last"]:
                o = oTs[gi % 2]
                P.op("vector", lambda e: e.tensor_copy(out=o[:], in_=ot[0:64, :]), r=[f"OT{gi % 2}"], w=[f"oTs{gi % 2}"])
                P.store("gpsimd", lambda e: e.dma_start(out=oT_d[h * 64:(h + 1) * 64, g * 512:(g + 1) * 512], in_=o[:]),
                        r=[f"oTs{gi % 2}"], dsem=osem[gi % 2])

        N = len(iters)
        ensure_loaded(0)
        for step in range(N + 2):
            if step < N:
                it = iters[step]
                if it["hh"] == 0 and it["g"] == 0 and it["first"]:
                    ensure_loaded(it["hp"] + 1)
                S1(it)
            if 0 <= step - 1 < N:
                S2(iters[step - 1])
            if 0 <= step - 2 < N:
                S5(iters[step - 2])
                S7(iters[step - 2])
            if 0 <= step - 1 < N:
                S2b(iters[step - 1])
                S4(iters[step - 1])
        P.drain(Ebs[0][:, 0:1])
        P.flush()


def phase_attn_diff(nc, P, sb, ps, l, S, jd, v_sb, qT_d, kT_d, oT_d, ident, mask2, neglam):
    ng = S // 256
    with ExitStack() as st:
        qTp = [[sb(st, f"dqTp{i}{m}", [128, S], BF16) for m in range(2)] for i in range(2)]
        kTp = [sb(st, f"dkTp{i}", [128, S], BF16) for i in range(2)]
        lsem = [P.new_sem(f"df_ld{l}_{i}") for i in range(2)]
        for i in range(2):
            P.op("gpsimd", lambda e, i=i: e.memset(qTp[i][0][64:128, :], 0.0), w=[f"dqTp{i}"])
            P.op("gpsimd", lambda e, i=i: e.memset(qTp[i][1][0:64, :], 0.0), w=[f"dqTp{i}"])
        Pb = [sb(st, f"Pb{i}", [128, 512], BF16) for i in range(3)]
        accs = [sb(st, f"accs{i}", [128, 2, 2, 130], F32) for i in range(2)]
        rc = [sb(st, f"rc{i}", [128, 2, 2], F32) for i in range(2)]
        od = [sb(st, f"od{i}", [128, 2, 128], F32) for i in range(2)]
        junk = sb(st, "junkd", [128, 128], F32)
        ssd = [sb(st, f"ssd{i}", [128, 2], F32) for i in range(2)]
        rsd = [sb(st, f"rsd{i}", [128, 2], F32) for i in range(2)]
        ob = [sb(st, f"ob{i}", [128, 2, 128], BF16) for i in range(2)]
        oTd = [sb(st, f"oTd{i}", [128, 256], BF16) for i in range(2)]
        epi = []
        osem = [P.new_sem(f"df_o{l}_{i}") for i in range(2)]
        Sb = [ps(st, f"Sb{i}", [128, 512], F32) for i in range(2)]
        acc = [[ps(st, f"acc{j}{m}", [128, 512], F32) for m in range(2)] for j in range(2)]
        Tb = ps(st, "Tb", [128, 2, 128], BF16)

        iters = []
        gcount = 0
        for h in range(8):
            for g in range(ng):
                for kb in range(2 * g + 1, -1, -1):
                    iters.append(dict(h=h, g=g, kb=kb, last=(kb == 0), gi=gcount,
                                      band=(kb - 2 * g) if kb >= 2 * g else -1))
                gcount += 1
        for n, it in enumerate(iters):
            it["n"] = n
        loaded = set()

        def ensure_loaded(h):
            if h in loaded or h >= 8:
                return
            loaded.add(h)
            s = h % 2
            P.op("sync", lambda e: e.dma_start(out=qTp[s][0][0:64, :], in_=qT_d[h * 128:h * 128 + 64, :]),
                 w=[f"dqTp{s}"], dsem=lsem[s])
            P.op("sync", lambda e: e.dma_start(out=qTp[s][1][64:128, :], in_=qT_d[h * 128 + 64:(h + 1) * 128, :]),
                 w=[f"dqTp{s}"], dsem=lsem[s])
            P.op("sync", lambda e: e.dma_start(out=kTp[s][:], in_=kT_d[h * 128:(h + 1) * 128, :]),
                 w=[f"dkTp{s}"], dsem=lsem[s])

        def S1(it):
            n, h, g, kb = it["n"], it["h"], it["g"], it["kb"]
            s = h % 2
            c0 = 0
            dc = 128 * it["band"] if it["band"] >= 0 else 0
            t0 = g * 256
            sbk = Sb[n % 2]
            band = it["band"] >= 0
            for m in range(2):
                kT = kTp[s][:, kb * 128:(kb + 1) * 128]
                qT = qTp[s][m][:, t0 + c0:t0 + 256]
                P.op("tensor", lambda e, m=m, kT=kT, qT=qT: e.matmul(sbk[:, m * 256 + c0:(m + 1) * 256], lhsT=kT, rhs=qT,
                                                                     start=True, stop=(not band)),
                     r=[f"dqTp{s}", f"dkTp{s}"], w=[f"Sb{n % 2}"], ev=(m == 1 and not band))
                if band:
                    P.op("tensor", lambda e, m=m: e.matmul(sbk[:, m * 256 + dc:m * 256 + dc + 128], lhsT=ident[:], rhs=mask2[:],
                                                           start=False, stop=True),
                         r=["ident", "mask2"], w=[f"Sb{n % 2}"], ev=(m == 1))

        def S2(it):
            n = it["n"]
            c0 = 0
            sbk = Sb[n % 2]
            pb = Pb[n % 3]
            if c0 == 0:
                P.op("scalar", lambda e: e.activation(out=pb[:], in_=sbk[:], func=AF.Exp), r=[f"Sb{n % 2}"], w=[f"Pb{n % 3}"])
            else:
                for m in range(2):
                    P.op("scalar", lambda e, m=m: e.activation(out=pb[:, m * 256 + c0:(m + 1) * 256],
                                                               in_=sbk[:, m * 256 + c0:(m + 1) * 256], func=AF.Exp),
                         r=[f"Sb{n % 2}"], w=[f"Pb{n % 3}"])

        def S3(it):
            n, h, g, kb, gi = it["n"], it["h"], it["g"], it["kb"], it["gi"]
            j0 = it["band"] if it["band"] >= 0 else 0
            pb = Pb[n % 3]
            for j in range(j0, 2):
                for m in range(2):
                    P.op("tensor", lambda e, j=j, m=m: e.matmul(acc[j][m][:, 0:130], lhsT=pb[:, m * 256 + j * 128:m * 256 + (j + 1) * 128],
                                                                rhs=v_sb[:, kb, h * 130:(h + 1) * 130],
                                                                start=(kb == 2 * g + j), stop=(kb == 0)),
                         r=["v_sb", f"Pb{n % 3}"], w=[f"acc{j}{m}"], ev=(kb == 0 or (j == 1 and m == 1)))
            if not it["last"] or DIFFDBG >= 1:
                return
            b = gi % 2
            while epi and epi[0][0] <= gi - 2:
                epi.pop(0)[1]()
            A, R_, O, SS, RS, OB, OT_ = accs[b], rc[b], od[b], ssd[b], rsd[b], ob[b], oTd[b]
            ka, kr, ko, kss, krs, kob, kot = f"accs{b}", f"rc{b}", f"od{b}", f"ssd{b}", f"rsd{b}", f"ob{b}", f"oTd{b}"
            for j in range(2):
                for m in range(2):
                    if m == 0:
                        P.op("vector", lambda e, j=j, m=m: e.tensor_copy(out=A[:, j, m, :], in_=acc[j][m][:, 0:130]),
                             r=[f"acc{j}{m}"], w=[ka])
                    else:
                        P.op("scalar", lambda e, j=j, m=m: e.copy(out=A[:, j, m, :], in_=acc[j][m][:, 0:130]),
                             r=[f"acc{j}{m}"], w=[ka])
            epi.append((gi, lambda: P.op("vector", lambda e: e.reciprocal(out=R_[:], in_=A[:, :, :, 128]), r=[ka], w=[kr])))
            epi.append((gi, lambda: P.op("vector", lambda e: e.tensor_tensor(out=R_[:, :, 1], in0=R_[:, :, 1],
                                                                       in1=neglam[:, jd:jd + 1].to_broadcast([128, 2]), op=ALU.mult),
                                    r=[kr, "neglam"], w=[kr])))
            for j in range(2):
                epi.append((gi, lambda j=j: P.op("vector", lambda e: e.tensor_scalar(out=O[:, j, :], in0=A[:, j, 0, 0:128], scalar1=R_[:, j, 0:1],
                                                                                scalar2=None, op0=ALU.mult), r=[ka, kr], w=[ko])))
                epi.append((gi, lambda j=j: P.op("vector", lambda e: e.scalar_tensor_tensor(out=O[:, j, :], in0=A[:, j, 1, 0:128],
                                                                                       scalar=R_[:, j, 1:2], in1=O[:, j, :],
                                                                                       op0=ALU.mult, op1=ALU.add), r=[ka, kr, ko], w=[ko])))
                epi.append((gi, lambda j=j: P.op("scalar", lambda e: e.activation(out=junk[:], in_=O[:, j, :], func=AF.Square,
                                                                             accum_out=SS[:, j:j + 1]), r=[ko], w=["junkd", kss])))
            epi.append((gi, lambda: P.op("vector", lambda e: e.tensor_scalar(out=RS[:], in0=SS[:], scalar1=1.0 / 128, scalar2=EPS,
                                                                        op0=ALU.mult, op1=ALU.add), r=[kss], w=[krs])))
            epi.append((gi, lambda: P.op("scalar", lambda e: e.activation(out=RS[:], in_=RS[:], func=AF.Ln), r=[krs], w=[krs])))
            epi.append((gi, lambda: P.op("scalar", lambda e: e.activation(out=RS[:], in_=RS[:], func=AF.Exp, scale=-0.5), r=[krs], w=[krs])))
            for j in range(2):
                epi.append((gi, lambda j=j: P.op("vector", lambda e: e.tensor_scalar(out=OB[:, j, :], in0=O[:, j, :], scalar1=RS[:, j:j + 1],
                                                                                scalar2=None, op0=ALU.mult), r=[ko, krs], w=[kob])))
            for j in range(2):
                epi.append((gi, lambda j=j: P.op("tensor", lambda e: e.transpose(out=Tb[:, j, :], in_=OB[:, j, :], identity=ident[:]),
                                            r=[kob, "ident"], w=["Tb"])))
            epi.append((gi, lambda: P.op("vector", lambda e: e.tensor_copy(out=OT_[:], in_=Tb[:].rearrange("p j t -> p (j t)")),
                                    r=["Tb"], w=[kot])))
            epi.append((gi, lambda: P.store("gpsimd", lambda e: e.dma_start(out=oT_d[h * 128:(h + 1) * 128, g * 256:(g + 1) * 256], in_=OT_[:]),
                                       r=[kot], dsem=osem[b])))

        N = len(iters)
        ensure_loaded(0)
        for step in range(N + 1):
            if step < N:
                it = iters[step]
                if it["g"] == 0 and it["kb"] == 1:
                    ensure_loaded(it["h"] + 1)
                if DIFFDBG < 4:
                    S1(it)
                if DIFFDBG < 3:
                    S2(it)
            if 0 <= step - 1 < N and DIFFDBG < 2:
                S3(iters[step - 1])
            for _ in range(2):
                if epi:
                    epi.pop(0)[1]()
        while epi:
            epi.pop(0)[1]()
        P.drain(junk[:, 0:1])
        P.flush()


def phase_ffn(nc, P, sb, ps, l, S, diff, jd, x_cur, x_next, oT_d, w_out, w_gate, w_up, w_down, g_ffn, subg, ident):
    GW = 256
    ng = S // GW
    nj = GW // 128
    with ExitStack() as st:
        wo = sb(st, "wo", [128, 8, D], BF16)
        wg = sb(st, "wg", [128, 8, DFF], BF16)
        wu = sb(st, "wu", [128, 8, DFF], BF16)
        wd = sb(st, "wd", [128, NFC, D], BF16)
        with ExitStack() as st2:
            stage = [sb(st2, f"fwst{i}", [128, 1024], F32) for i in range(6)]
            ssem = [P.new_sem(f"ffn_ws{l}_{i}") for i in range(6)]
            load_weight(P, wo, stage, ssem, lambda c: w_out[l, c * 128:(c + 1) * 128, :], 8, D,
                        (lambda c: subg[:, jd:jd + 1]) if diff else None, "wo")
            load_weight(P, wg, stage, ssem, lambda c: w_gate[l, c * 128:(c + 1) * 128, :], 8, DFF,
                        lambda c: g_ffn[:, l, c:c + 1], "wg")
            load_weight(P, wu, stage, ssem, lambda c: w_up[l, c * 128:(c + 1) * 128, :], 8, DFF,
                        lambda c: g_ffn[:, l, c:c + 1], "wu")
            load_weight(P, wd, stage, ssem, lambda c: w_down[l, c * 128:(c + 1) * 128, :], NFC, D, None, "wd")
            P.flush()
        oTs = [sb(st, f"foT{i}", [128, 8, GW], BF16) for i in range(2)]
        olsem = [P.new_sem(f"ffn_ol{l}_{i}") for i in range(2)]
        xg = [[sb(st, f"xg{s}{j}", [128, D], F32) for j in range(nj)] for s in range(2)]
        xlsem = [[P.new_sem(f"ffn_xl{l}_{s}{j}") for j in range(nj)] for s in range(2)]
        xssem = [[P.new_sem(f"ffn_xs{l}_{s}{j}") for j in range(nj)] for s in range(2)]
        junk = sb(st, "junkf", [128, D], BF16)
        ss = sb(st, "ssf", [128, 2], F32)
        rstd = sb(st, "rstdf", [128, 2], F32)
        hb = [sb(st, f"hbf{i}", [128, D], BF16) for i in range(2)]
        h2T = sb(st, "h2T", [128, 8, GW], BF16)
        aT = sb(st, "aT", [128, NFC, GW], BF16)
        sg = [sb(st, f"sg{i}", [128, GW], F32) for i in range(2)]
        pO = [ps(st, f"pO{i}", [128, 512], F32) for i in range(1)]
        pT = ps(st, "pTf", [128, 8, 128], BF16)
        pG = [ps(st, f"pG{i}", [128, GW], F32) for i in range(2)]
        pU = [ps(st, f"pU{i}", [128, GW], F32) for i in range(2)]
        pD = [ps(st, f"pD{i}", [128, 512], F32) for i in range(2)]

        kO = 0
        kD = 0
        kh = 0
        for g in range(ng):
            s = g % 2
            t0 = g * GW
            P.op("sync", lambda e, s=s, t0=t0: e.dma_start(out=oTs[s][:], in_=oT_d[:, t0:t0 + GW].rearrange("(c p) t -> p c t", p=128)),
                 w=[f"foT{s}"], dsem=olsem[s])
            for j in range(nj):
                P.op("sync", lambda e, s=s, j=j, t0=t0: e.dma_start(out=xg[s][j][:], in_=x_cur[t0 + j * 128:t0 + (j + 1) * 128, :]),
                     w=[f"xg{s}{j}"], dsem=xlsem[s][j])
            for j in range(nj):
                for half in range(2):
                    po = pO[0]
                    pk = "pO0"
                    kO += 1
                    for c in range(8):
                        P.op("tensor", lambda e, c=c, j=j, half=half, po=po, s=s: e.matmul(
                            po[:], lhsT=oTs[s][:, c, j * 128:(j + 1) * 128], rhs=wo[:, c, half * 512:(half + 1) * 512],
                            start=(c == 0), stop=(c == 7)), r=[f"foT{s}", "wo"], w=[pk], ev=(c == 7))
                    xa = xg[s][j][:, half * 512:(half + 1) * 512]
                    P.op("vector", lambda e, xa=xa, po=po: e.tensor_tensor(out=xa, in0=xa, in1=po[:], op=ALU.add),
                         r=[pk, f"xg{s}{j}"], w=[f"xg{s}{j}"])
            for j in range(nj):
                hs = kh % 2
                kh += 1
                tag = f"f{hs}"
                rmsnorm_tile(P, xg[s][j][:], f"xg{s}{j}", junk[:], ss[:, hs:hs + 1], rstd[:, hs:hs + 1], hb[hs][:], tag)
                for c in range(8):
                    P.op("tensor", lambda e, c=c, hs=hs: e.transpose(out=pT[:, c, :], in_=hb[hs][:, c * 128:(c + 1) * 128],
                                                                    identity=ident[:]),
                         r=["hb" + tag, "ident"], w=["pTf"], ev=(c == 7))
                P.op("scalar", lambda e, j=j: e.copy(out=h2T[:, :, j * 128:(j + 1) * 128], in_=pT[:]), r=["pTf"], w=["h2T"])
            for fc in range(NFC):
                b = fc % 2
                for c in range(8):
                    P.op("tensor", lambda e, c=c, fc=fc, b=b: e.matmul(pG[b][:], lhsT=wg[:, c, fc * 128:(fc + 1) * 128], rhs=h2T[:, c, :],
                                                                      start=(c == 0), stop=(c == 7)),
                         r=["wg", "h2T"], w=[f"pG{b}"], ev=(c == 7))
                for c in range(8):
                    P.op("tensor", lambda e, c=c, fc=fc, b=b: e.matmul(pU[b][:], lhsT=wu[:, c, fc * 128:(fc + 1) * 128], rhs=h2T[:, c, :],
                                                                      start=(c == 0), stop=(c == 7)),
                         r=["wu", "h2T"], w=[f"pU{b}"], ev=(c == 7))
                P.op("scalar", lambda e, b=b: e.activation(out=sg[b][:], in_=pG[b][:], func=AF.Silu), r=[f"pG{b}"], w=[f"sg{b}"])
                P.op("vector", lambda e, b=b, fc=fc: e.tensor_tensor(out=aT[:, fc, :], in0=sg[b][:], in1=pU[b][:], op=ALU.mult),
                     r=[f"sg{b}", f"pU{b}"], w=["aT"])
            for j in range(nj):
                for half in range(2):
                    pd = pD[kD % 2]
                    pk = f"pD{kD % 2}"
                    kD += 1
                    for fc in range(NFC):
                        P.op("tensor", lambda e, fc=fc, j=j, half=half, pd=pd: e.matmul(
                            pd[:], lhsT=aT[:, fc, j * 128:(j + 1) * 128], rhs=wd[:, fc, half * 512:(half + 1) * 512],
                            start=(fc == 0), stop=(fc == NFC - 1)), r=["aT", "wd"], w=[pk], ev=(fc == NFC - 1))
                    xa = xg[s][j][:, half * 512:(half + 1) * 512]
                    P.op("vector", lambda e, xa=xa, pd=pd: e.tensor_tensor(out=xa, in0=xa, in1=pd[:], op=ALU.add),
                         r=[pk, f"xg{s}{j}"], w=[f"xg{s}{j}"])
                P.store("gpsimd", lambda e, s=s, j=j, t0=t0: e.dma_start(out=x_next[t0 + j * 128:t0 + (j + 1) * 128, :], in_=xg[s][j][:]),
                        r=[f"xg{s}{j}"], dsem=xssem[s][j])
        P.drain(junk[:, 0:1])
        P.flush()


_NC_CACHE = {}


def _get_nc(S, layers=NL, debug=False):
    key = (S, layers, debug)
    if key not in _NC_CACHE:
        _NC_CACHE[key] = build_nc(S, layers, debug)
    return _NC_CACHE[key]


def kernel(x, positions, attn_norm, w_in, w_out, q_norm, k_norm, lambda_q1, lambda_k1,
           lambda_q2, lambda_k2, sub_norm, ffn_norm, w_gate, w_up, w_down):
    x = np.asarray(x)
    B, S, _ = x.shape
    nc = _get_nc(S)
    shared = dict(attn_norm=attn_norm, w_in=w_in, w_out=w_out, q_norm=q_norm, k_norm=k_norm,
                  lambda_q1=lambda_q1, lambda_k1=lambda_k1, lambda_q2=lambda_q2, lambda_k2=lambda_k2,
                  sub_norm=sub_norm, ffn_norm=ffn_norm, w_gate=w_gate, w_up=w_up, w_down=w_down)
    shared = {k: np.ascontiguousarray(np.asarray(v, dtype=np.float32)) for k, v in shared.items()}
    in_maps = []
    for b in range(B):
        m = dict(shared)
        m["x"] = np.ascontiguousarray(x[b], dtype=np.float32)
        m["positions"] = np.ascontiguousarray(np.asarray(positions)[b], dtype=np.int32)
        in_maps.append(m)
    res = run_bass_kernel_spmd(nc, in_maps, core_ids=list(range(B)))
    return np.stack([np.asarray(r["out"]) for r in res.results], axis=0).astype(np.float32)
```

```python
import math
import os
import re
from contextlib import ExitStack

import numpy as np
import concourse.bass as bass
import concourse.mybir as mybir
from concourse.bass_utils import run_bass_kernel_spmd

F32 = mybir.dt.float32
BF16 = mybir.dt.bfloat16
I32 = mybir.dt.int32
AF = mybir.ActivationFunctionType
ALU = mybir.AluOpType
AX = mybir.AxisListType

D = 1024
DFF = 2816
NFC = DFF // 128
NL = 4
EPS = 1e-6
NEG = -30000.0
ROPE_THETA = 500000.0
ENGS = ("sync", "scalar", "vector", "gpsimd", "tensor")
SAME_ENG_SYNC = True
DIFFDBG = int(os.environ.get('DIFFDBG', '0'))


class Sem:
    def __init__(self, h, name):
        self.h = h
        self.v = 0
        self.name = name


class Ev:
    __slots__ = ("sem", "val")

    def __init__(self, sem=None, val=None):
        self.sem = sem
        self.val = val


class Prog:
    def __init__(self, nc, stack):
        self.nc = nc
        self.stack = stack
        self.sems = {}
        self.nst = 0
        self.store_keys = []
        self.esem = {e: self.new_sem("ev_" + e) for e in ENGS}
        self.q = {e: [] for e in ENGS}
        self.lastw = {}
        self.readers = {}
        self.pending = {e: [] for e in ENGS}
        self.waited = {e: {} for e in ENGS}
        self.nops = 0

    def new_sem(self, name):
        name = re.sub(r"\d+_", "_", name, count=1) if name.count("_") >= 2 else name
        if name in self.sems:
            return self.sems[name]
        h = self.stack.enter_context(self.nc.semaphore(name))
        s = Sem(h, name)
        self.sems[name] = s
        return s

    def store(self, eng, fn, r, dsem):
        key = f"__st{self.nst}"
        self.nst += 1
        self.op(eng, fn, r=r, w=[key], dsem=dsem)
        self.store_keys.append(key)

    def drain(self, scratch_ap):
        keys = list(self.store_keys)
        self.store_keys = []
        self.op("gpsimd", lambda e: e.memset(scratch_ap, 0.0), r=keys, w=["__drain"])

    def op(self, eng, fn, r=(), w=(), ev=True, dsem=None):
        deps = []
        for b in r:
            e = self.lastw.get(b)
            if e is not None:
                deps.append(e)
        for b in w:
            e = self.lastw.get(b)
            if e is not None:
                deps.append(e)
            deps.extend(self.readers.get(b, ()))
        pend = self.pending[eng]
        if pend:
            deps = [d for d in deps if not (d.sem is None and any(d is p for p in pend))]
        n = 0
        if dsem is not None:
            dsem.v += 16
            event = Ev(dsem, dsem.v)
            n = 16
        elif ev:
            s = self.esem[eng]
            s.v += 1
            event = Ev(s, s.v)
            n = 1
            for p in self.pending[eng]:
                p.sem = s
                p.val = s.v
            self.pending[eng] = []
        else:
            event = Ev()
            self.pending[eng].append(event)
        self.q[eng].append((fn, deps, event, n))
        for b in w:
            self.lastw[b] = event
            self.readers[b] = []
        for b in r:
            self.readers.setdefault(b, []).append(event)
        self.nops += 1

    def flush(self):
        nc = self.nc
        for e in ENGS:
            assert not self.pending[e], f"pending events on {e}"
        with nc.Block() as block:
            for eng in ENGS:
                ops = self.q[eng]
                if not ops:
                    continue

                def body(e, ops=ops, eng=eng):
                    waited = self.waited[eng]
                    own = self.esem[eng]
                    for fn, deps, event, n in ops:
                        for d in deps:
                            s, v = d.sem, d.val
                            assert s is not None
                            if s is own and not SAME_ENG_SYNC:
                                continue
                            if waited.get(s, 0) >= v:
                                continue
                            e.wait_ge(s.h, v)
                            waited[s] = v
                        ins = fn(e)
                        if n:
                            ins.then_inc(event.sem.h, n)

                getattr(block, eng)(body)
        self.q = {e: [] for e in ENGS}


def build_nc(S, layers=NL, debug=False, nph=99):
    nt = S // 128
    nc = bass.Bass("TRN2", target_bir_lowering=False)
    okind = "ExternalOutput" if debug else "Internal"

    def din(name, shape, dt=F32):
        return nc.dram_tensor(name, list(shape), dt, kind="ExternalInput").ap()

    x_in = din("x", [S, D])
    pos_in = din("positions", [S], I32)
    attn_norm = din("attn_norm", [NL, D])
    w_in = din("w_in", [NL, D, 3 * D])
    w_out = din("w_out", [NL, D, D])
    q_norm = din("q_norm", [2, 64])
    k_norm = din("k_norm", [2, 64])
    lq1 = din("lambda_q1", [2, 64])
    lk1 = din("lambda_k1", [2, 64])
    lq2 = din("lambda_q2", [2, 64])
    lk2 = din("lambda_k2", [2, 64])
    sub_norm = din("sub_norm", [2, 128])
    ffn_norm = din("ffn_norm", [NL, D])
    w_gate = din("w_gate", [NL, D, DFF])
    w_up = din("w_up", [NL, D, DFF])
    w_down = din("w_down", [NL, DFF, D])
    out = nc.dram_tensor("out", [S, D], F32, kind="ExternalOutput").ap()
    xs = [nc.dram_tensor(f"xs{i}", [S, D], F32, kind=okind).ap() for i in range(2)]
    qT_d = nc.dram_tensor("qT_d", [D, S], BF16, kind=okind).ap()
    kT_d = nc.dram_tensor("kT_d", [D, S], BF16, kind=okind).ap()
    oT_d = nc.dram_tensor("oT_d", [D, S], BF16, kind=okind).ap()

    with ExitStack() as gs, nc.allow_low_precision("bf16 matmul operands, fp32 accumulation"):
        P = Prog(nc, gs)

        uid = [0]

        def sb(stack, name, shape, dt):
            uid[0] += 1
            return stack.enter_context(nc.sbuf_tensor(f"{name}_u{uid[0]}", list(shape), dt))

        def ps(stack, name, shape, dt):
            uid[0] += 1
            return stack.enter_context(nc.psum_tensor(f"{name}_u{uid[0]}", list(shape), dt))

        ident = sb(gs, "ident", [128, 128], BF16)
        negones = sb(gs, "negones", [128, 128], BF16)
        ntri = sb(gs, "ntri", [128, 128], BF16)
        maskneg = sb(gs, "maskneg", [128, 128], BF16)
        mask2 = sb(gs, "mask2", [128, 128], BF16)
        g_attn = sb(gs, "g_attn", [128, NL, 8], F32)
        g_ffn = sb(gs, "g_ffn", [128, NL, 8], F32)
        cosT = sb(gs, "cosT", [128, nt, 8], F32)
        sinT = sb(gs, "sinT", [128, nt, 8], F32)
        gq = sb(gs, "gq", [128, 2, 64], F32)
        gk = sb(gs, "gk", [128, 2, 64], F32)
        neglam = sb(gs, "neglam", [128, 2], F32)
        subg = sb(gs, "subg", [128, 2], F32)
        mpi = sb(gs, "mpi", [128, 1], F32)
        csem = P.new_sem("csem")
        ts = ExitStack()
        onesb = sb(ts, "onesb", [128, 128], BF16)
        negbig = sb(ts, "negbig", [128, 128], BF16)
        posi = sb(ts, "posi", [128, nt], I32)
        posf = sb(ts, "posf", [128, nt], F32)
        ang = sb(ts, "ang", [128, nt, 8], F32)
        ang2 = sb(ts, "ang2", [128, nt, 8], F32)
        lam_in = sb(ts, "lam_in", [128, 8, 64], F32)
        lam_tmp = sb(ts, "lam_tmp", [128, 4, 64], F32)
        lam_s = sb(ts, "lam_s", [128, 4], F32)
        lam_e = sb(ts, "lam_e", [128, 4], F32)

        V, G, A, T, SY = "vector", "gpsimd", "scalar", "tensor", "sync"

        P.op(G, lambda e: e.memset(onesb[:], 1.0), w=["onesb"])
        P.op(G, lambda e: e.memset(negones[:], -1.0), w=["negones"])
        P.op(G, lambda e: e.memset(negbig[:], NEG), w=["negbig"])
        P.op(G, lambda e: e.memset(mpi[:], -math.pi), w=["mpi"])
        P.op(G, lambda e: e.affine_select(out=ident[:], in_=onesb[:], pattern=[[1, 128]],
                                          compare_op=ALU.is_equal, fill=0.0, base=0,
                                          channel_multiplier=-1), r=["onesb"], w=["ident"])
        P.op(G, lambda e: e.affine_select(out=ntri[:], in_=negones[:], pattern=[[-1, 128]],
                                          compare_op=ALU.is_ge, fill=0.0, base=0,
                                          channel_multiplier=1), r=["negones"], w=["ntri"])
        P.op(G, lambda e: e.affine_select(out=maskneg[:], in_=negbig[:], pattern=[[-1, 128]],
                                          compare_op=ALU.is_ge, fill=0.0, base=0,
                                          channel_multiplier=1), r=["negbig"], w=["maskneg"])
        P.op(G, lambda e: e.memset(mask2[:], 0.0), w=["mask2"])
        P.op(G, lambda e: e.memset(mask2[64:128, 0:64], NEG), w=["mask2"])

        def small_dma(dst, src, key):
            P.op(SY, lambda e: e.dma_start(out=dst, in_=src), w=["consts"], dsem=csem)

        identF = sb(ts, "identF", [128, 128], F32)
        P.op(V, lambda e: e.tensor_copy(out=identF[:], in_=ident[:]), r=["ident"], w=["identF"])
        rows = sb(ts, "c_rows", [2 * NL * 8 + nt + 2, 128], F32)
        posr = sb(ts, "c_posr", [nt, 128], I32)
        NR = 2 * NL * 8
        small_dma(rows[0:NL * 8, :], attn_norm.rearrange("l (c p) -> (l c) p", p=128), "rows")
        small_dma(rows[NL * 8:NR, :], ffn_norm.rearrange("l (c p) -> (l c) p", p=128), "rows")
        small_dma(posr[:], pos_in.rearrange("(t p) -> t p", p=128), "posr")
        for j in range(2):
            small_dma(gq[:, j, :], q_norm[j, :].partition_broadcast(128), "gq")
            small_dma(gk[:, j, :], k_norm[j, :].partition_broadcast(128), "gk")
            for i, lt in enumerate((lq1, lk1, lq2, lk2)):
                small_dma(lam_in[:, j * 4 + i, :], lt[j, :].partition_broadcast(128), "lam_in")
        subr = sb(ts, "c_subr", [2, 128], F32)
        small_dma(subr[:], sub_norm, "subr")
        posrf = sb(ts, "c_posrf", [nt, 128], F32)
        P.op(V, lambda e: e.tensor_copy(out=posrf[:], in_=posr[:]), r=["consts"], w=["posrf"])
        with ExitStack() as cst:
            pc = ps(cst, "pc", [128, 512], F32)
            P.op(T, lambda e: e.transpose(out=pc[:, 0:NR], in_=rows[0:NR, :], identity=identF[0:NR, 0:NR]),
                 r=["consts", "identF"], w=["pc"])
            P.op(T, lambda e: e.transpose(out=pc[:, 64:64 + nt], in_=posrf[:], identity=identF[0:nt, 0:nt]),
                 r=["posrf", "identF"], w=["pc"])
            P.op(T, lambda e: e.transpose(out=pc[:, 128:130], in_=subr[:], identity=identF[0:2, 0:2]),
                 r=["consts", "identF"], w=["pc"])
            P.op(V, lambda e: e.tensor_copy(out=g_attn[:].rearrange("p l c -> p (l c)"), in_=pc[:, 0:NL * 8]), r=["pc"], w=["g_attn"])
            P.op(V, lambda e: e.tensor_copy(out=g_ffn[:].rearrange("p l c -> p (l c)"), in_=pc[:, NL * 8:NR]), r=["pc"], w=["g_ffn"])
            P.op(V, lambda e: e.tensor_copy(out=posf[:], in_=pc[:, 64:64 + nt]), r=["pc"], w=["posf"])
            P.op(V, lambda e: e.tensor_copy(out=subg[:], in_=pc[:, 128:130]), r=["pc"], w=["subg"])
            P.flush()

        inv_freq = (np.float32(ROPE_THETA) ** (-np.arange(0, 16, 2, dtype=np.float32) / np.float32(16))).astype(np.float32)
        for i in range(8):
            P.op(V, lambda e, i=i: e.tensor_scalar(out=ang[:, :, i], in0=posf[:], scalar1=float(inv_freq[i]),
                                                   scalar2=None, op0=ALU.mult), r=["posf"], w=["ang"], ev=(i == 7))
        twopi = 2.0 * math.pi
        C1 = 6.28125
        C2 = twopi - C1
        ki = sb(ts, "rope_ki", [128, nt, 8], I32)
        kf = sb(ts, "rope_kf", [128, nt, 8], F32)
        mk = sb(ts, "rope_mk", [128, nt, 8], F32)

        def vop(fn, r, w):
            P.op(V, fn, r=r, w=w)

        vop(lambda e: e.tensor_scalar(out=kf[:], in0=ang[:], scalar1=1.0 / twopi, scalar2=None, op0=ALU.mult), ["ang"], ["kf"])
        vop(lambda e: e.tensor_copy(out=ki[:], in_=kf[:]), ["kf"], ["ki"])
        vop(lambda e: e.tensor_copy(out=kf[:], in_=ki[:]), ["ki"], ["kf"])
        vop(lambda e: e.scalar_tensor_tensor(out=ang2[:], in0=kf[:], scalar=-C1, in1=ang[:], op0=ALU.mult, op1=ALU.add),
            ["kf", "ang"], ["ang2"])
        vop(lambda e: e.scalar_tensor_tensor(out=ang2[:], in0=kf[:], scalar=-C2, in1=ang2[:], op0=ALU.mult, op1=ALU.add),
            ["kf", "ang2"], ["ang2"])

        def wrap(buf, key):
            vop(lambda e: e.tensor_scalar(out=mk[:], in0=buf[:], scalar1=math.pi, scalar2=None, op0=ALU.is_gt), [key], ["mk"])
            vop(lambda e: e.scalar_tensor_tensor(out=buf[:], in0=mk[:], scalar=-twopi, in1=buf[:], op0=ALU.mult, op1=ALU.add),
                ["mk", key], [key])
            vop(lambda e: e.tensor_scalar(out=mk[:], in0=buf[:], scalar1=-math.pi, scalar2=None, op0=ALU.is_lt), [key], ["mk"])
            vop(lambda e: e.scalar_tensor_tensor(out=buf[:], in0=mk[:], scalar=twopi, in1=buf[:], op0=ALU.mult, op1=ALU.add),
                ["mk", key], [key])
            vop(lambda e: e.tensor_scalar(out=buf[:], in0=buf[:], scalar1=math.pi, scalar2=-math.pi, op0=ALU.min, op1=ALU.max),
                [key], [key])

        wrap(ang2, "ang2")
        P.op(A, lambda e: e.activation(out=sinT[:], in_=ang2[:], func=AF.Sin), r=["ang2"], w=["sinT"])
        vop(lambda e: e.tensor_scalar(out=ang[:], in0=ang2[:], scalar1=math.pi / 2, scalar2=None, op0=ALU.add), ["ang2", "ang"], ["ang"])
        wrap(ang, "ang")
        P.op(A, lambda e: e.activation(out=cosT[:], in_=ang[:], func=AF.Sin), r=["ang"], w=["cosT"])
        for j in range(2):
            for i in range(2):
                k = j * 2 + i
                P.op(V, lambda e, j=j, i=i, k=k: e.tensor_tensor(out=lam_tmp[:, k, :], in0=lam_in[:, j * 4 + 2 * i, :],
                                                               in1=lam_in[:, j * 4 + 2 * i + 1, :], op=ALU.mult),
                     r=["consts"], w=["lam_tmp"])
        P.op(V, lambda e: e.tensor_reduce(out=lam_s[:], in_=lam_tmp[:], axis=AX.X, op=ALU.add),
             r=["lam_tmp"], w=["lam_s"])
        P.op(A, lambda e: e.activation(out=lam_e[:], in_=lam_s[:], func=AF.Exp), r=["lam_s"], w=["lam_e"])
        for j in range(2):
            lam_init = 0.8 - 0.6 * math.exp(-0.3 * (2 * j + 1))
            P.op(V, lambda e, j=j: e.tensor_tensor(out=neglam[:, j:j + 1], in0=lam_e[:, 2 * j + 1:2 * j + 2],
                                                   in1=lam_e[:, 2 * j:2 * j + 1], op=ALU.subtract),
                 r=["lam_e"], w=["neglam"])
            P.op(V, lambda e, j=j, li=lam_init: e.tensor_scalar(out=neglam[:, j:j + 1], in0=neglam[:, j:j + 1],
                                                                scalar1=-li, scalar2=None, op0=ALU.add),
                 r=["neglam"], w=["neglam"])
            P.op(V, lambda e, j=j, li=lam_init: e.tensor_scalar(out=subg[:, j:j + 1], in0=subg[:, j:j + 1],
                                                                scalar1=(1.0 - li), scalar2=None, op0=ALU.mult),
                 r=["subg"], w=["subg"])
        if debug:
            dcos = nc.dram_tensor("dbg_cos", [128, nt * 8], F32, kind="ExternalOutput").ap()
            dsin = nc.dram_tensor("dbg_sin", [128, nt * 8], F32, kind="ExternalOutput").ap()
            dmisc = nc.dram_tensor("dbg_misc", [128, 4 + 2 * NL * 8], F32, kind="ExternalOutput").ap()
            P.store(SY, lambda e: e.dma_start(out=dcos, in_=cosT[:].rearrange("p t i -> p (t i)")), r=["cosT"], dsem=csem)
            P.store(SY, lambda e: e.dma_start(out=dsin, in_=sinT[:].rearrange("p t i -> p (t i)")), r=["sinT"], dsem=csem)
            P.store(SY, lambda e: e.dma_start(out=dmisc[:, 0:2], in_=neglam[:]), r=["neglam"], dsem=csem)
            P.store(SY, lambda e: e.dma_start(out=dmisc[:, 2:4], in_=subg[:]), r=["subg"], dsem=csem)
            P.store(SY, lambda e: e.dma_start(out=dmisc[:, 4:4 + NL * 8], in_=g_attn[:].rearrange("p l c -> p (l c)")), r=["g_attn"], dsem=csem)
            P.store(SY, lambda e: e.dma_start(out=dmisc[:, 4 + NL * 8:4 + 2 * NL * 8], in_=g_ffn[:].rearrange("p l c -> p (l c)")), r=["g_ffn"], dsem=csem)
            P.drain(mpi[:, 0:1])
        P.flush()
        ts.close()

        x_cur = x_in
        for l in range(layers):
            diff = (l % 2 == 1)
            jd = l // 2
            x_next = out if l == layers - 1 else xs[l % 2]
            with ExitStack() as ls:
                vw = 8 * 130 if diff else D
                v_sb = sb(ls, f"v_sb{l}", [128, nt, vw], BF16)
                if diff:
                    for h in range(8):
                        P.op(G, lambda e, h=h: e.memset(v_sb[:, :, h * 130 + 128:h * 130 + 130], 1.0), w=["v_sb"])
                if nph >= 1:
                    phase_qkv(nc, P, sb, ps, l, S, diff, jd, x_cur, w_in, g_attn, v_sb, qT_d, kT_d, ident,
                              gq, gk, cosT, sinT)
                if nph < 2:
                    pass
                elif diff:
                    phase_attn_diff(nc, P, sb, ps, l, S, jd, v_sb, qT_d, kT_d, oT_d, ident, mask2, neglam)
                else:
                    phase_attn_sb(nc, P, sb, ps, l, S, v_sb, qT_d, kT_d, oT_d, ident, ntri, negones, maskneg)
            if nph >= 3:
                phase_ffn(nc, P, sb, ps, l, S, diff, jd, x_cur, x_next, oT_d, w_out, w_gate, w_up, w_down,
                          g_ffn, subg, ident)
            x_cur = x_next
    return nc


def load_weight(P, sb_w, stage, ssem, dram_rows, nchunk, width, scale_ap_fn, key, step=1024, extra_r=("g_ffn", "subg")):
    nst = len(stage)
    k = 0
    for c in range(nchunk):
        for o in range(0, width, step):
            wdt = min(step, width - o)
            s = k % nst
            k += 1
            P.op("sync", lambda e, c=c, o=o, wdt=wdt, s=s: e.dma_start(out=stage[s][:, 0:wdt],
                                                                      in_=dram_rows(c)[:, o:o + wdt]),
                 w=[f"wst{s}"], dsem=ssem[s])
            sc = scale_ap_fn(c) if scale_ap_fn is not None else None
            eng = "vector" if (k % 2 == 0) else "scalar"
            dst = sb_w[:, c, o:o + wdt]
            srcap = stage[s][:, 0:wdt]
            rr = [f"wst{s}"] + (list(extra_r) if sc is not None else [])
            if eng == "vector":
                if sc is None:
                    P.op(eng, lambda e, dst=dst, srcap=srcap: e.tensor_copy(out=dst, in_=srcap), r=rr, w=[key])
                else:
                    P.op(eng, lambda e, dst=dst, srcap=srcap, sc=sc: e.tensor_scalar(out=dst, in0=srcap, scalar1=sc, scalar2=None,
                                                                                   op0=ALU.mult), r=rr, w=[key])
            else:
                if sc is None:
                    P.op(eng, lambda e, dst=dst, srcap=srcap: e.copy(out=dst, in_=srcap), r=rr, w=[key])
                else:
                    P.op(eng, lambda e, dst=dst, srcap=srcap, sc=sc: e.activation(out=dst, in_=srcap, func=AF.Copy, scale=sc),
                         r=rr, w=[key])


def rmsnorm_tile(P, xt_ap, xkey, junk, ss, rstd, hb, tag, n=D):
    P.op("scalar", lambda e: e.activation(out=junk, in_=xt_ap, func=AF.Square, accum_out=ss),
         r=[xkey], w=["junk" + tag, "ss" + tag])
    P.op("vector", lambda e: e.tensor_scalar(out=rstd, in0=ss, scalar1=1.0 / n, scalar2=EPS,
                                             op0=ALU.mult, op1=ALU.add), r=["ss" + tag], w=["rstd" + tag])
    P.op("scalar", lambda e: e.activation(out=rstd, in_=rstd, func=AF.Ln), r=["rstd" + tag], w=["rstd" + tag])
    P.op("scalar", lambda e: e.activation(out=rstd, in_=rstd, func=AF.Exp, scale=-0.5), r=["rstd" + tag], w=["rstd" + tag])
    P.op("vector", lambda e: e.tensor_scalar(out=hb, in0=xt_ap, scalar1=rstd, scalar2=None, op0=ALU.mult),
         r=[xkey, "rstd" + tag, "junk" + tag], w=["hb" + tag])


def phase_qkv(nc, P, sb, ps, l, S, diff, jd, x_cur, w_in, g_attn, v_sb, qT_d, kT_d, ident, gq, gk, cosT, sinT):
    nt = S // 128
    with ExitStack() as st:
        wq = sb(st, "wq", [128, 8, 3 * D], BF16)
        stage = [sb(st, f"wst{i}", [128, 1024], F32) for i in range(2)]
        ssem = [P.new_sem(f"qkv_ws{l}_{i}") for i in range(2)]
        xt = [sb(st, f"xt{i}", [128, D], F32) for i in range(2)]
        xsem = [P.new_sem(f"qkv_x{l}_{i}") for i in range(2)]
        junk = sb(st, "junkq", [128, D], BF16)
        ss = sb(st, "ssq", [128, 2], F32)
        rstd = sb(st, "rstdq", [128, 2], F32)
        hb = [sb(st, f"hbq{i}", [128, D], BF16) for i in range(2)]
        hTs = [sb(st, f"hT{i}", [128, 8, 128], BF16) for i in range(2)]
        qk_toks = [sb(st, f"qk_tok{i}", [128, 2 * D], BF16) for i in range(2)]
        qkT = [sb(st, f"qkT{i}", [128, 16, 512], BF16) for i in range(2)]
        osem = [P.new_sem(f"qkv_o{l}_{i}") for i in range(2)]
        if diff:
            qkf = sb(st, "qkf", [128, 2 * D], F32)
            sq2 = sb(st, "sq2", [128, 2 * D], F32)
            ssq = sb(st, "ssq2", [128, 32], F32)
            rs = sb(st, "rs", [128, 32], F32)
            rt = [sb(st, f"rt{i}", [128, 32, 8], F32) for i in range(4)]
        pqkv = [ps(st, f"pqkv{i}", [128, 512], F32) for i in range(6)]
        pT = [ps(st, f"pT{i}", [128, 8, 128], BF16) for i in range(2)]

        load_weight(P, wq, stage, ssem, lambda c: w_in[l, c * 128:(c + 1) * 128, :], 8, 3 * D,
                    lambda c: g_attn[:, l, c:c + 1], "wq", extra_r=["g_attn"])

        def front_a(tt):
            s = tt % 2
            hT = hTs[tt % 2]
            HT = f"hT{tt % 2}"
            P.op("sync", lambda e, tt=tt, s=s: e.dma_start(out=xt[s][:], in_=x_cur[tt * 128:(tt + 1) * 128, :]),
                 w=[f"xt{s}"], dsem=xsem[s])
            tag = f"q{s}"
            rmsnorm_tile(P, xt[s][:], f"xt{s}", junk[:], ss[:, s:s + 1], rstd[:, s:s + 1], hb[s][:], tag)
            for c in range(8):
                P.op("tensor", lambda e, c=c, s=s: e.transpose(out=pT[0][:, c, :], in_=hb[s][:, c * 128:(c + 1) * 128],
                                                              identity=ident[:]),
                     r=["hb" + tag, "ident"], w=["pT0"], ev=(c == 7))
            P.op("scalar", lambda e: e.copy(out=hT[:], in_=pT[0][:]), r=["pT0"], w=[HT])

        def front_b(tt):
            s = tt % 2
            j4 = tt % 4
            gi = tt // 4
            qs = gi % 2
            qk_tok = qk_toks[tt % 2]
            QK = f"qk_tok{tt % 2}"
            hT = hTs[tt % 2]
            HT = f"hT{tt % 2}"
            for r_ in range(6):
                for c in range(8):
                    P.op("tensor", lambda e, c=c, r_=r_: e.matmul(pqkv[r_][:], lhsT=hT[:, c, :],
                                                                  rhs=wq[:, c, r_ * 512:(r_ + 1) * 512],
                                                                  start=(c == 0), stop=(c == 7)),
                         r=[HT, "wq"], w=[f"pqkv{r_}"], ev=(c == 7))
            if not diff:
                for r_ in range(2):
                    P.op("scalar", lambda e, r_=r_: e.mul(out=qk_tok[:, r_ * 512:(r_ + 1) * 512], in_=pqkv[r_][:], mul=0.125),
                         r=[f"pqkv{r_}"], w=[QK])
                for r_ in range(2, 4):
                    P.op("vector", lambda e, r_=r_: e.tensor_copy(out=qk_tok[:, r_ * 512:(r_ + 1) * 512], in_=pqkv[r_][:]),
                         r=[f"pqkv{r_}"], w=[QK])
                P.op("scalar", lambda e, tt=tt: e.copy(out=v_sb[:, tt, 0:512], in_=pqkv[4][:]), r=["pqkv4"], w=["v_sb"])
                P.op("vector", lambda e, tt=tt: e.tensor_copy(out=v_sb[:, tt, 512:1024], in_=pqkv[5][:]),
                     r=["pqkv5"], w=["v_sb"])
            else:
                for r_ in range(4):
                    P.op("scalar", lambda e, r_=r_: e.activation(out=sq2[:, r_ * 512:(r_ + 1) * 512], in_=pqkv[r_][:],
                                                                 func=AF.Square), r=[f"pqkv{r_}"], w=["sq2"])
                P.op("vector", lambda e: e.tensor_reduce(out=ssq[:], in_=sq2[:].rearrange("p (g d) -> p g d", d=64),
                                                         axis=AX.X, op=ALU.add), r=["sq2"], w=["ssq"])
                P.op("vector", lambda e: e.tensor_scalar(out=rs[:], in0=ssq[:], scalar1=1.0 / 64, scalar2=EPS,
                                                         op0=ALU.mult, op1=ALU.add), r=["ssq"], w=["rs"])
                P.op("scalar", lambda e: e.activation(out=rs[:], in_=rs[:], func=AF.Ln), r=["rs"], w=["rs"])
                P.op("scalar", lambda e: e.activation(out=rs[:], in_=rs[:], func=AF.Exp, scale=-0.5), r=["rs"], w=["rs"])
                P.op("vector", lambda e: e.tensor_scalar(out=rs[:, 0:16], in0=rs[:, 0:16], scalar1=0.125, scalar2=None,
                                                         op0=ALU.mult), r=["rs"], w=["rs"])
                for r_ in range(4):
                    P.op("vector", lambda e, r_=r_: e.tensor_tensor(
                        out=qkf[:, r_ * 512:(r_ + 1) * 512].rearrange("p (g d) -> p g d", d=64),
                        in0=pqkv[r_][:].rearrange("p (g d) -> p g d", d=64),
                        in1=rs[:, r_ * 8:(r_ + 1) * 8].unsqueeze(2).to_broadcast([128, 8, 64]), op=ALU.mult),
                         r=[f"pqkv{r_}", "rs"], w=["qkf"])
                for half, gt in ((0, gq), (1, gk)):
                    P.op("vector", lambda e, half=half, gt=gt: e.tensor_tensor(
                        out=qkf[:, half * D:(half + 1) * D].rearrange("p (g d) -> p g d", d=64),
                        in0=qkf[:, half * D:(half + 1) * D].rearrange("p (g d) -> p g d", d=64),
                        in1=gt[:, jd, :].unsqueeze(1).to_broadcast([128, 16, 64]), op=ALU.mult),
                         r=["qkf", "consts"], w=["qkf"])
                qv = qkf[:].rearrange("p (g d) -> p g d", d=64)
                qo = qk_tok[:].rearrange("p (g d) -> p g d", d=64)
                cb = cosT[:, tt, :].unsqueeze(1).to_broadcast([128, 32, 8])
                sbb = sinT[:, tt, :].unsqueeze(1).to_broadcast([128, 32, 8])
                x1 = qv[:, :, 0:8]
                x2 = qv[:, :, 8:16]
                P.op("vector", lambda e, x1=x1, cb=cb: e.tensor_tensor(out=rt[0][:], in0=x1, in1=cb, op=ALU.mult),
                     r=["qkf", "cosT"], w=["rt0"])
                P.op("vector", lambda e, x2=x2, sbb=sbb: e.tensor_tensor(out=rt[1][:], in0=x2, in1=sbb, op=ALU.mult),
                     r=["qkf", "sinT"], w=["rt1"])
                P.op("vector", lambda e, x2=x2, cb=cb: e.tensor_tensor(out=rt[2][:], in0=x2, in1=cb, op=ALU.mult),
                     r=["qkf", "cosT"], w=["rt2"])
                P.op("vector", lambda e, x1=x1, sbb=sbb: e.tensor_tensor(out=rt[3][:], in0=x1, in1=sbb, op=ALU.mult),
                     r=["qkf", "sinT"], w=["rt3"])
                P.op("vector", lambda e, qo=qo: e.tensor_tensor(out=qo[:, :, 0:8], in0=rt[0][:], in1=rt[1][:], op=ALU.subtract),
                     r=["rt0", "rt1"], w=[QK])
                P.op("vector", lambda e, qo=qo: e.tensor_tensor(out=qo[:, :, 8:16], in0=rt[2][:], in1=rt[3][:], op=ALU.add),
                     r=["rt2", "rt3"], w=[QK])
                P.op("vector", lambda e, qo=qo, qv=qv: e.tensor_copy(out=qo[:, :, 16:64], in_=qv[:, :, 16:64]),
                     r=["qkf"], w=[QK])
                vv = v_sb[:, tt, :].rearrange("p (h d) -> p h d", d=130)
                P.op("scalar", lambda e, vv=vv: e.copy(out=vv[:, 0:4, 0:128], in_=pqkv[4][:].rearrange("p (h d) -> p h d", d=128)),
                     r=["pqkv4"], w=["v_sb"])
                P.op("vector", lambda e, vv=vv: e.tensor_copy(out=vv[:, 4:8, 0:128], in_=pqkv[5][:].rearrange("p (h d) -> p h d", d=128)),
                     r=["pqkv5"], w=["v_sb"])
        def back(tt):
            s = tt % 2
            j4 = tt % 4
            gi = tt // 4
            qs = gi % 2
            qk_tok = qk_toks[tt % 2]
            QK = f"qk_tok{tt % 2}"
            for half in range(2):
                for c in range(8):
                    cc = half * 8 + c
                    P.op("tensor", lambda e, cc=cc, c=c: e.transpose(out=pT[1][:, c, :], in_=qk_tok[:, cc * 128:(cc + 1) * 128],
                                                                    identity=ident[:]),
                         r=[QK, "ident"], w=["pT1"], ev=(c == 7))
                eng = "scalar" if half == 0 else "vector"
                if eng == "scalar":
                    P.op(eng, lambda e, half=half, j4=j4, qs=qs: e.copy(out=qkT[qs][:, half * 8:(half + 1) * 8, j4 * 128:(j4 + 1) * 128],
                                                                        in_=pT[1][:]), r=["pT1"], w=[f"qkT{qs}"])
                else:
                    P.op(eng, lambda e, half=half, j4=j4, qs=qs: e.tensor_copy(out=qkT[qs][:, half * 8:(half + 1) * 8, j4 * 128:(j4 + 1) * 128],
                                                                               in_=pT[1][:]), r=["pT1"], w=[f"qkT{qs}"])
            if j4 == 3 or tt == nt - 1:
                t0 = gi * 512
                wdt = (j4 + 1) * 128
                P.store("gpsimd", lambda e, t0=t0, wdt=wdt, qs=qs: e.dma_start(
                    out=qT_d[:, t0:t0 + wdt].rearrange("(c p) t -> p c t", p=128), in_=qkT[qs][:, 0:8, 0:wdt]),
                        r=[f"qkT{qs}"], dsem=osem[qs])
                P.store("gpsimd", lambda e, t0=t0, wdt=wdt, qs=qs: e.dma_start(
                    out=kT_d[:, t0:t0 + wdt].rearrange("(c p) t -> p c t", p=128), in_=qkT[qs][:, 8:16, 0:wdt]),
                        r=[f"qkT{qs}"], dsem=osem[qs])
        for tt in range(nt + 1):
            if tt < nt:
                front_a(tt)
            if tt >= 1:
                back(tt - 1)
            if tt < nt:
                front_b(tt)
        P.drain(junk[:, 0:1])
        P.flush()


def phase_attn_sb(nc, P, sb, ps, l, S, v_sb, qT_d, kT_d, oT_d, ident, ntri, negones, maskneg):
    GW = min(1024, S)
    NTG = GW // 128
    ng = S // GW
    with ExitStack() as st:
        qTp = [sb(st, f"qTp{i}", [128, S], BF16) for i in range(2)]
        kTp = [sb(st, f"kTp{i}", [128, S], BF16) for i in range(2)]
        lsem = [P.new_sem(f"sb_ld{l}_{i}") for i in range(2)]
        Ebs = [sb(st, f"Eb{i}", [128, GW], F32) for i in range(2)]
        SPb = [sb(st, f"SPb{i}", [128, GW], BF16) for i in range(2)]
        R32 = [sb(st, f"R32_{i}", [128, GW], F32) for i in range(2)]
        Rb = [sb(st, f"Rb{i}", [128, GW], BF16) for i in range(2)]
        Wb = [sb(st, f"Wb{i}", [128, GW], BF16) for i in range(2)]
        oTs = [sb(st, f"oTs{i}", [64, GW], BF16) for i in range(2)]
        osem = [P.new_sem(f"sb_o{l}_{i}") for i in range(2)]
        Zb = ps(st, "Zb", [128, GW], F32)
        Lb = [ps(st, f"Lb{i}", [128, GW], F32) for i in range(2)]
        OT = ps(st, "OT", [128, GW], F32)

        iters = []
        gcount = 0
        for hp in range(8):
            for hh in range(2):
                for g in range(ng):
                    kbs = list(range(NTG * g + NTG - 1, -1, -1))
                    for idx, kb in enumerate(kbs):
                        iters.append(dict(hp=hp, hh=hh, g=g, kb=kb, first=(idx == 0), last=(kb == 0), gi=gcount,
                                          band=(kb - NTG * g) if kb >= NTG * g else -1))
                    gcount += 1
        for n, it in enumerate(iters):
            it["n"] = n
        loaded = set()
        ot_started = {}

        def ensure_loaded(hp):
            if hp in loaded or hp >= 8:
                return
            loaded.add(hp)
            s = hp % 2
            P.op("sync", lambda e: e.dma_start(out=qTp[s][:], in_=qT_d[hp * 128:(hp + 1) * 128, :]),
                 w=[f"qTp{s}"], dsem=lsem[s])
            P.op("sync", lambda e: e.dma_start(out=kTp[s][:], in_=kT_d[hp * 128:(hp + 1) * 128, :]),
                 w=[f"kTp{s}"], dsem=lsem[s])

        def cols(it):
            c0 = 128 * it["band"] if it["band"] >= 0 else 0
            return c0, GW

        def pieces(a, b):
            out = []
            while a < b:
                nb = min(b, (a // 512 + 1) * 512)
                out.append((a, nb))
                a = nb
            return out

        def mm_qk(it, dst, key, final_ev):
            n, hp, hh, g, kb = it["n"], it["hp"], it["hh"], it["g"], it["kb"]
            s = hp % 2
            pb = hh * 64
            c0, c1 = cols(it)
            t0 = g * GW
            band = it["band"] >= 0
            kT = kTp[s][pb:pb + 64, kb * 128:(kb + 1) * 128]
            pcs = pieces(c0, c1)
            for i, (a, b) in enumerate(pcs):
                lastp = (i == len(pcs) - 1)
                qT = qTp[s][pb:pb + 64, t0 + a:t0 + b]
                P.op("tensor", lambda e, a=a, b=b, qT=qT: e.matmul(dst[:, a:b], lhsT=kT, rhs=qT, start=True, stop=False,
                                                                  skip_group_check=True),
                     r=[f"qTp{s}", f"kTp{s}"], w=[key], ev=(final_ev and lastp and not band))
            if band:
                P.op("tensor", lambda e: e.matmul(dst[:, c0:c0 + 128], lhsT=ident[:], rhs=maskneg[:], start=False, stop=False,
                                                  skip_group_check=True),
                     r=["ident", "maskneg"], w=[key], ev=final_ev)

        def MM1(it):
            mm_qk(it, Zb, "Zb", True)

        def MM2(it):
            n = it["n"]
            mm_qk(it, Lb[n % 2], f"Lb{n % 2}", False)

        def EXP(it):
            n = it["n"]
            c0, c1 = cols(it)
            eb = Ebs[n % 2]
            P.op("scalar", lambda e: e.activation(out=eb[:, c0:c1], in_=Zb[:, c0:c1], func=AF.Exp),
                 r=["Zb"], w=[f"Eb{n % 2}"])

        def LN(it):
            n = it["n"]
            c0, c1 = cols(it)
            sp = SPb[n % 2]
            eb = Ebs[n % 2]
            P.op("scalar", lambda e: e.activation(out=sp[:, c0:c1], in_=eb[:, c0:c1], func=AF.Ln, bias=1.0, scale=1.0),
                 r=[f"Eb{n % 2}"], w=[f"SPb{n % 2}"])
            if it["last"]:
                return
            gi = it["gi"]
            r32 = R32[gi % 2]
            rb = Rb[n % 2]
            if it["band"] >= 0:
                P.op("vector", lambda e: e.tensor_copy(out=r32[:, c0:c0 + 128], in_=sp[:, c0:c0 + 128]),
                     r=[f"SPb{n % 2}"], w=[f"R32_{gi % 2}"])
                if c0 + 128 < GW:
                    P.op("vector", lambda e: e.tensor_tensor(out=r32[:, c0 + 128:GW], in0=r32[:, c0 + 128:GW],
                                                             in1=sp[:, c0 + 128:GW], op=ALU.add),
                         r=[f"SPb{n % 2}", f"R32_{gi % 2}"], w=[f"R32_{gi % 2}"])
            else:
                P.op("vector", lambda e: e.tensor_tensor(out=r32[:], in0=r32[:], in1=sp[:], op=ALU.add),
                     r=[f"SPb{n % 2}", f"R32_{gi % 2}"], w=[f"R32_{gi % 2}"])
            P.op("vector", lambda e: e.tensor_copy(out=rb[:, c0:c1], in_=r32[:, c0:c1]),
                 r=[f"R32_{gi % 2}"], w=[f"Rb{n % 2}"])

        def MM43(it):
            n = it["n"]
            c0, c1 = cols(it)
            lb = Lb[n % 2]
            sp = SPb[n % 2]
            if not it["first"]:
                r0 = c0 + 128 if it["band"] >= 0 else 0
                rb = Rb[(n - 1) % 2]
                for (a, b) in pieces(r0, c1):
                    P.op("tensor", lambda e, a=a, b=b: e.matmul(lb[:, a:b], lhsT=negones[:], rhs=rb[:, a:b], start=False, stop=False,
                                                                skip_group_check=True),
                         r=["negones", f"Rb{(n - 1) % 2}"], w=[f"Lb{n % 2}"], ev=False)
            pcs = pieces(c0, c1)
            for i, (a, b) in enumerate(pcs):
                P.op("tensor", lambda e, a=a, b=b: e.matmul(lb[:, a:b], lhsT=ntri[:], rhs=sp[:, a:b], start=False, stop=True,
                                                            skip_group_check=True),
                     r=["ntri", f"SPb{n % 2}"], w=[f"Lb{n % 2}"], ev=(i == len(pcs) - 1))

        def EXPW(it):
            n = it["n"]
            c0, c1 = cols(it)
            lb = Lb[n % 2]
            wb = Wb[n % 2]
            P.op("scalar", lambda e: e.activation(out=wb[:, c0:c1], in_=lb[:, c0:c1], func=AF.Exp),
                 r=[f"Lb{n % 2}"], w=[f"Wb{n % 2}"])

        def PV(it):
            n, hp, hh, g, kb, gi = it["n"], it["hp"], it["hh"], it["g"], it["kb"], it["gi"]
            h = 2 * hp + hh
            c0, c1 = cols(it)
            wb = Wb[n % 2]
            pcs = pieces(c0, c1)
            for i, (a, b) in enumerate(pcs):
                bank = a // 512
                st_ = (gi, bank) not in ot_started
                ot_started[(gi, bank)] = True
                P.op("tensor", lambda e, a=a, b=b, st_=st_: e.matmul(OT[0:64, a:b], lhsT=v_sb[:, kb, h * 64:(h + 1) * 64], rhs=wb[:, a:b],
                                                                    start=st_, stop=it["last"], skip_group_check=True),
                     r=["v_sb", f"Wb{n % 2}"], w=["OT"], ev=(i == len(pcs) - 1))
            if it["last"]:
                o = oTs[gi % 2]
                hw = GW // 2
                P.op("vector", lambda e: e.tensor_copy(out=o[:, 0:hw], in_=OT[0:64, 0:hw]), r=["OT"], w=[f"oTs{gi % 2}"])
                P.op("vector", lambda e: e.tensor_copy(out=o[:, hw:GW], in_=OT[0:64, hw:GW]), r=["OT"], w=[f"oTs{gi % 2}"])
                P.store("gpsimd", lambda e: e.dma_start(out=oT_d[h * 64:(h + 1) * 64, g * GW:(g + 1) * GW], in_=o[:]),
                        r=[f"oTs{gi % 2}"], dsem=osem[gi % 2])

        N = len(iters)
        ensure_loaded(0)

        def at(i):
            return iters[i] if 0 <= i < N else None

        for k in range(-1, N + 1):
            nxt, cur, prv = at(k + 1), at(k), at(k - 1)
            if nxt is not None and nxt["hh"] == 0 and nxt["g"] == 0 and nxt["first"]:
                ensure_loaded(nxt["hp"] + 1)
            if cur is not None:
                EXP(cur)
            if nxt is not None:
                MM1(nxt)
            if cur is not None:
                LN(cur)
                MM43(cur)
            if prv is not None:
                EXPW(prv)
                PV(prv)
            if nxt is not None:
                MM2(nxt)
        P.drain(Ebs[0][:, 0:1])
        P.flush()


def phase_attn_diff(nc, P, sb, ps, l, S, jd, v_sb, qT_d, kT_d, oT_d, ident, mask2, neglam):
    ng = S // 256
    with ExitStack() as st:
        qTp = [[sb(st, f"dqTp{i}{m}", [128, S], BF16) for m in range(2)] for i in range(2)]
        kTp = [sb(st, f"dkTp{i}", [128, S], BF16) for i in range(2)]
        lsem = [P.new_sem(f"df_ld{l}_{i}") for i in range(2)]
        for i in range(2):
            P.op("gpsimd", lambda e, i=i: e.memset(qTp[i][0][64:128, :], 0.0), w=[f"dqTp{i}"])
            P.op("gpsimd", lambda e, i=i: e.memset(qTp[i][1][0:64, :], 0.0), w=[f"dqTp{i}"])
        Pb = [sb(st, f"Pb{i}", [128, 512], BF16) for i in range(3)]
        accs = [sb(st, f"accs{i}", [128, 2, 2, 130], F32) for i in range(2)]
        rc = [sb(st, f"rc{i}", [128, 2, 2], F32) for i in range(2)]
        od = [sb(st, f"od{i}", [128, 2, 128], F32) for i in range(2)]
        junk = sb(st, "junkd", [128, 128], F32)
        ssd = [sb(st, f"ssd{i}", [128, 2], F32) for i in range(2)]
        rsd = [sb(st, f"rsd{i}", [128, 2], F32) for i in range(2)]
        ob = [sb(st, f"ob{i}", [128, 2, 128], BF16) for i in range(2)]
        oTd = [sb(st, f"oTd{i}", [128, 256], BF16) for i in range(2)]
        epi = []
        osem = [P.new_sem(f"df_o{l}_{i}") for i in range(2)]
        Sb = [ps(st, f"Sb{i}", [128, 512], F32) for i in range(2)]
        acc = [[ps(st, f"acc{j}{m}", [128, 512], F32) for m in range(2)] for j in range(2)]
        Tb = ps(st, "Tb", [128, 2, 128], BF16)

        iters = []
        gcount = 0
        for h in range(8):
            for g in range(ng):
                for kb in range(2 * g + 1, -1, -1):
                    iters.append(dict(h=h, g=g, kb=kb, last=(kb == 0), gi=gcount,
                                      band=(kb - 2 * g) if kb >= 2 * g else -1))
                gcount += 1
        for n, it in enumerate(iters):
            it["n"] = n
        loaded = set()

        def ensure_loaded(h):
            if h in loaded or h >= 8:
                return
            loaded.add(h)
            s = h % 2
            P.op("sync", lambda e: e.dma_start(out=qTp[s][0][0:64, :], in_=qT_d[h * 128:h * 128 + 64, :]),
                 w=[f"dqTp{s}"], dsem=lsem[s])
            P.op("sync", lambda e: e.dma_start(out=qTp[s][1][64:128, :], in_=qT_d[h * 128 + 64:(h + 1) * 128, :]),
                 w=[f"dqTp{s}"], dsem=lsem[s])
            P.op("sync", lambda e: e.dma_start(out=kTp[s][:], in_=kT_d[h * 128:(h + 1) * 128, :]),
                 w=[f"dkTp{s}"], dsem=lsem[s])

        def S1(it):
            n, h, g, kb = it["n"], it["h"], it["g"], it["kb"]
            s = h % 2
            c0 = 0
            dc = 128 * it["band"] if it["band"] >= 0 else 0
            t0 = g * 256
            sbk = Sb[n % 2]
            band = it["band"] >= 0
            for m in range(2):
                kT = kTp[s][:, kb * 128:(kb + 1) * 128]
                qT = qTp[s][m][:, t0 + c0:t0 + 256]
                P.op("tensor", lambda e, m=m, kT=kT, qT=qT: e.matmul(sbk[:, m * 256 + c0:(m + 1) * 256], lhsT=kT, rhs=qT,
                                                                     start=True, stop=(not band)),
                     r=[f"dqTp{s}", f"dkTp{s}"], w=[f"Sb{n % 2}"], ev=(m == 1 and not band))
                if band:
                    P.op("tensor", lambda e, m=m: e.matmul(sbk[:, m * 256 + dc:m * 256 + dc + 128], lhsT=ident[:], rhs=mask2[:],
                                                           start=False, stop=True),
                         r=["ident", "mask2"], w=[f"Sb{n % 2}"], ev=(m == 1))

        def S2(it):
            n = it["n"]
            c0 = 0
            sbk = Sb[n % 2]
            pb = Pb[n % 3]
            if c0 == 0:
                P.op("scalar", lambda e: e.activation(out=pb[:], in_=sbk[:], func=AF.Exp), r=[f"Sb{n % 2}"], w=[f"Pb{n % 3}"])
            else:
                for m in range(2):
                    P.op("scalar", lambda e, m=m: e.activation(out=pb[:, m * 256 + c0:(m + 1) * 256],
                                                               in_=sbk[:, m * 256 + c0:(m + 1) * 256], func=AF.Exp),
                         r=[f"Sb{n % 2}"], w=[f"Pb{n % 3}"])

        def S3(it):
            n, h, g, kb, gi = it["n"], it["h"], it["g"], it["kb"], it["gi"]
            j0 = it["band"] if it["band"] >= 0 else 0
            pb = Pb[n % 3]
            for j in range(j0, 2):
                for m in range(2):
                    P.op("tensor", lambda e, j=j, m=m: e.matmul(acc[j][m][:, 0:130], lhsT=pb[:, m * 256 + j * 128:m * 256 + (j + 1) * 128],
                                                                rhs=v_sb[:, kb, h * 130:(h + 1) * 130],
                                                                start=(kb == 2 * g + j), stop=(kb == 0)),
                         r=["v_sb", f"Pb{n % 3}"], w=[f"acc{j}{m}"], ev=(kb == 0 or (j == 1 and m == 1)))
            if not it["last"] or DIFFDBG >= 1:
                return
            b = gi % 2
            while epi and epi[0][0] <= gi - 2:
                epi.pop(0)[1]()
            A, R_, O, SS, RS, OB, OT_ = accs[b], rc[b], od[b], ssd[b], rsd[b], ob[b], oTd[b]
            ka, kr, ko, kss, krs, kob, kot = f"accs{b}", f"rc{b}", f"od{b}", f"ssd{b}", f"rsd{b}", f"ob{b}", f"oTd{b}"
            for j in range(2):
                for m in range(2):
                    if m == 0:
                        P.op("vector", lambda e, j=j, m=m: e.tensor_copy(out=A[:, j, m, :], in_=acc[j][m][:, 0:130]),
                             r=[f"acc{j}{m}"], w=[ka])
                    else:
                        P.op("scalar", lambda e, j=j, m=m: e.copy(out=A[:, j, m, :], in_=acc[j][m][:, 0:130]),
                             r=[f"acc{j}{m}"], w=[ka])
            epi.append((gi, lambda: P.op("vector", lambda e: e.reciprocal(out=R_[:], in_=A[:, :, :, 128]), r=[ka], w=[kr])))
            epi.append((gi, lambda: P.op("vector", lambda e: e.tensor_tensor(out=R_[:, :, 1], in0=R_[:, :, 1],
                                                                       in1=neglam[:, jd:jd + 1].to_broadcast([128, 2]), op=ALU.mult),
                                    r=[kr, "neglam"], w=[kr])))
            for j in range(2):
                epi.append((gi, lambda j=j: P.op("vector", lambda e: e.tensor_scalar(out=O[:, j, :], in0=A[:, j, 0, 0:128], scalar1=R_[:, j, 0:1],
                                                                                scalar2=None, op0=ALU.mult), r=[ka, kr], w=[ko])))
                epi.append((gi, lambda j=j: P.op("vector", lambda e: e.scalar_tensor_tensor(out=O[:, j, :], in0=A[:, j, 1, 0:128],
                                                                                       scalar=R_[:, j, 1:2], in1=O[:, j, :],
                                                                                       op0=ALU.mult, op1=ALU.add), r=[ka, kr, ko], w=[ko])))
                epi.append((gi, lambda j=j: P.op("scalar", lambda e: e.activation(out=junk[:], in_=O[:, j, :], func=AF.Square,
                                                                             accum_out=SS[:, j:j + 1]), r=[ko], w=["junkd", kss])))
            epi.append((gi, lambda: P.op("vector", lambda e: e.tensor_scalar(out=RS[:], in0=SS[:], scalar1=1.0 / 128, scalar2=EPS,
                                                                        op0=ALU.mult, op1=ALU.add), r=[kss], w=[krs])))
            epi.append((gi, lambda: P.op("scalar", lambda e: e.activation(out=RS[:], in_=RS[:], func=AF.Ln), r=[krs], w=[krs])))
            epi.append((gi, lambda: P.op("scalar", lambda e: e.activation(out=RS[:], in_=RS[:], func=AF.Exp, scale=-0.5), r=[krs], w=[krs])))
            for j in range(2):
                epi.append((gi, lambda j=j: P.op("vector", lambda e: e.tensor_scalar(out=OB[:, j, :], in0=O[:, j, :], scalar1=RS[:, j:j + 1],
                                                                                scalar2=None, op0=ALU.mult), r=[ko, krs], w=[kob])))
            for j in range(2):
                epi.append((gi, lambda j=j: P.op("tensor", lambda e: e.transpose(out=Tb[:, j, :], in_=OB[:, j, :], identity=ident[:]),
                                            r=[kob, "ident"], w=["Tb"])))
            epi.append((gi, lambda: P.op("vector", lambda e: e.tensor_copy(out=OT_[:], in_=Tb[:].rearrange("p j t -> p (j t)")),
                                    r=["Tb"], w=[kot])))
            epi.append((gi, lambda: P.store("gpsimd", lambda e: e.dma_start(out=oT_d[h * 128:(h + 1) * 128, g * 256:(g + 1) * 256], in_=OT_[:]),
                                       r=[kot], dsem=osem[b])))

        N = len(iters)
        ensure_loaded(0)
        for step in range(N + 1):
            if step < N:
                it = iters[step]
                if it["g"] == 0 and it["kb"] == 1:
                    ensure_loaded(it["h"] + 1)
                if DIFFDBG < 4:
                    S1(it)
                if DIFFDBG < 3:
                    S2(it)
            if 0 <= step - 1 < N and DIFFDBG < 2:
                S3(iters[step - 1])
            for _ in range(2):
                if epi:
                    epi.pop(0)[1]()
        while epi:
            epi.pop(0)[1]()
        P.drain(junk[:, 0:1])
        P.flush()


def phase_ffn(nc, P, sb, ps, l, S, diff, jd, x_cur, x_next, oT_d, w_out, w_gate, w_up, w_down, g_ffn, subg, ident):
    GW = 256
    ng = S // GW
    nj = GW // 128
    with ExitStack() as st:
        wo = sb(st, "wo", [128, 8, D], BF16)
        wg = sb(st, "wg", [128, 8, DFF], BF16)
        wu = sb(st, "wu", [128, 8, DFF], BF16)
        wd = sb(st, "wd", [128, NFC, D], BF16)
        with ExitStack() as st2:
            stage = [sb(st2, f"fwst{i}", [128, 1024], F32) for i in range(4)]
            ssem = [P.new_sem(f"ffn_ws{l}_{i}") for i in range(4)]
            load_weight(P, wo, stage, ssem, lambda c: w_out[l, c * 128:(c + 1) * 128, :], 8, D,
                        (lambda c: subg[:, jd:jd + 1]) if diff else None, "wo")
            load_weight(P, wg, stage, ssem, lambda c: w_gate[l, c * 128:(c + 1) * 128, :], 8, DFF,
                        lambda c: g_ffn[:, l, c:c + 1], "wg")
            load_weight(P, wu, stage, ssem, lambda c: w_up[l, c * 128:(c + 1) * 128, :], 8, DFF,
                        lambda c: g_ffn[:, l, c:c + 1], "wu")
            load_weight(P, wd, stage, ssem, lambda c: w_down[l, c * 128:(c + 1) * 128, :], NFC, D, None, "wd")
            P.flush()
        oTs = [sb(st, f"foT{i}", [128, 8, GW], BF16) for i in range(2)]
        olsem = [P.new_sem(f"ffn_ol{l}_{i}") for i in range(2)]
        xg = [[sb(st, f"xg{s}{j}", [128, D], F32) for j in range(nj)] for s in range(2)]
        xlsem = [[P.new_sem(f"ffn_xl{l}_{s}{j}") for j in range(nj)] for s in range(2)]
        xssem = [[P.new_sem(f"ffn_xs{l}_{s}{j}") for j in range(nj)] for s in range(2)]
        junk = sb(st, "junkf", [128, D], BF16)
        ss = sb(st, "ssf", [128, 2], F32)
        rstd = sb(st, "rstdf", [128, 2], F32)
        hb = [sb(st, f"hbf{i}", [128, D], BF16) for i in range(2)]
        h2T = sb(st, "h2T", [128, 8, GW], BF16)
        aT = sb(st, "aT", [128, NFC, GW], BF16)
        sg = [sb(st, f"sg{i}", [128, GW], F32) for i in range(2)]
        pO = [ps(st, f"pO{i}", [128, 512], F32) for i in range(1)]
        pT = ps(st, "pTf", [128, 8, 128], BF16)
        pG = [ps(st, f"pG{i}", [128, GW], F32) for i in range(2)]
        pU = [ps(st, f"pU{i}", [128, GW], F32) for i in range(2)]
        pD = [ps(st, f"pD{i}", [128, 512], F32) for i in range(2)]

        kO = 0
        kD = 0
        kh = 0
        for g in range(ng):
            s = g % 2
            t0 = g * GW
            P.op("sync", lambda e, s=s, t0=t0: e.dma_start(out=oTs[s][:], in_=oT_d[:, t0:t0 + GW].rearrange("(c p) t -> p c t", p=128)),
                 w=[f"foT{s}"], dsem=olsem[s])
            for j in range(nj):
                P.op("sync", lambda e, s=s, j=j, t0=t0: e.dma_start(out=xg[s][j][:], in_=x_cur[t0 + j * 128:t0 + (j + 1) * 128, :]),
                     w=[f"xg{s}{j}"], dsem=xlsem[s][j])
            for j in range(nj):
                for half in range(2):
                    po = pO[0]
                    pk = "pO0"
                    kO += 1
                    for c in range(8):
                        P.op("tensor", lambda e, c=c, j=j, half=half, po=po, s=s: e.matmul(
                            po[:], lhsT=oTs[s][:, c, j * 128:(j + 1) * 128], rhs=wo[:, c, half * 512:(half + 1) * 512],
                            start=(c == 0), stop=(c == 7)), r=[f"foT{s}", "wo"], w=[pk], ev=(c == 7))
                    xa = xg[s][j][:, half * 512:(half + 1) * 512]
                    P.op("vector", lambda e, xa=xa, po=po: e.tensor_tensor(out=xa, in0=xa, in1=po[:], op=ALU.add),
                         r=[pk, f"xg{s}{j}"], w=[f"xg{s}{j}"])
            for j in range(nj):
                hs = kh % 2
                kh += 1
                tag = f"f{hs}"
                rmsnorm_tile(P, xg[s][j][:], f"xg{s}{j}", junk[:], ss[:, hs:hs + 1], rstd[:, hs:hs + 1], hb[hs][:], tag)
                for c in range(8):
                    P.op("tensor", lambda e, c=c, hs=hs: e.transpose(out=pT[:, c, :], in_=hb[hs][:, c * 128:(c + 1) * 128],
                                                                    identity=ident[:]),
                         r=["hb" + tag, "ident"], w=["pTf"], ev=(c == 7))
                P.op("scalar", lambda e, j=j: e.copy(out=h2T[:, :, j * 128:(j + 1) * 128], in_=pT[:]), r=["pTf"], w=["h2T"])
            for fc in range(NFC):
                b = fc % 2
                for c in range(8):
                    P.op("tensor", lambda e, c=c, fc=fc, b=b: e.matmul(pG[b][:], lhsT=wg[:, c, fc * 128:(fc + 1) * 128], rhs=h2T[:, c, :],
                                                                      start=(c == 0), stop=(c == 7)),
                         r=["wg", "h2T"], w=[f"pG{b}"], ev=(c == 7))
                for c in range(8):
                    P.op("tensor", lambda e, c=c, fc=fc, b=b: e.matmul(pU[b][:], lhsT=wu[:, c, fc * 128:(fc + 1) * 128], rhs=h2T[:, c, :],
                                                                      start=(c == 0), stop=(c == 7)),
                         r=["wu", "h2T"], w=[f"pU{b}"], ev=(c == 7))
                P.op("scalar", lambda e, b=b: e.activation(out=sg[b][:], in_=pG[b][:], func=AF.Silu), r=[f"pG{b}"], w=[f"sg{b}"])
                P.op("vector", lambda e, b=b, fc=fc: e.tensor_tensor(out=aT[:, fc, :], in0=sg[b][:], in1=pU[b][:], op=ALU.mult),
                     r=[f"sg{b}", f"pU{b}"], w=["aT"])
            for j in range(nj):
                for half in range(2):
                    pd = pD[kD % 2]
                    pk = f"pD{kD % 2}"
                    kD += 1
                    for fc in range(NFC):
                        P.op("tensor", lambda e, fc=fc, j=j, half=half, pd=pd: e.matmul(
                            pd[:], lhsT=aT[:, fc, j * 128:(j + 1) * 128], rhs=wd[:, fc, half * 512:(half + 1) * 512],
                            start=(fc == 0), stop=(fc == NFC - 1)), r=["aT", "wd"], w=[pk], ev=(fc == NFC - 1))
                    xa = xg[s][j][:, half * 512:(half + 1) * 512]
                    P.op("vector", lambda e, xa=xa, pd=pd: e.tensor_tensor(out=xa, in0=xa, in1=pd[:], op=ALU.add),
                         r=[pk, f"xg{s}{j}"], w=[f"xg{s}{j}"])
                P.store("gpsimd", lambda e, s=s, j=j, t0=t0: e.dma_start(out=x_next[t0 + j * 128:t0 + (j + 1) * 128, :], in_=xg[s][j][:]),
                        r=[f"xg{s}{j}"], dsem=xssem[s][j])
        P.drain(junk[:, 0:1])
        P.flush()


_NC_CACHE = {}


def _get_nc(S, layers=NL, debug=False):
    key = (S, layers, debug)
    if key not in _NC_CACHE:
        _NC_CACHE[key] = build_nc(S, layers, debug)
    return _NC_CACHE[key]


def kernel(x, positions, attn_norm, w_in, w_out, q_norm, k_norm, lambda_q1, lambda_k1,
           lambda_q2, lambda_k2, sub_norm, ffn_norm, w_gate, w_up, w_down):
    x = np.asarray(x)
    B, S, _ = x.shape
    nc = _get_nc(S)
    shared = dict(attn_norm=attn_norm, w_in=w_in, w_out=w_out, q_norm=q_norm, k_norm=k_norm,
                  lambda_q1=lambda_q1, lambda_k1=lambda_k1, lambda_q2=lambda_q2, lambda_k2=lambda_k2,
                  sub_norm=sub_norm, ffn_norm=ffn_norm, w_gate=w_gate, w_up=w_up, w_down=w_down)
    shared = {k: np.ascontiguousarray(np.asarray(v, dtype=np.float32)) for k, v in shared.items()}
    in_maps = []
    for b in range(B):
        m = dict(shared)
        m["x"] = np.ascontiguousarray(x[b], dtype=np.float32)
        m["positions"] = np.ascontiguousarray(np.asarray(positions)[b], dtype=np.int32)
        in_maps.append(m)
    res = run_bass_kernel_spmd(nc, in_maps, core_ids=list(range(B)))
    return np.stack([np.asarray(r["out"]) for r in res.results], axis=0).astype(np.float32)
```

```python
import math
import os
import re
from contextlib import ExitStack

import numpy as np
import concourse.bass as bass
import concourse.mybir as mybir
from concourse.bass_utils import run_bass_kernel_spmd

F32 = mybir.dt.float32
BF16 = mybir.dt.bfloat16
I32 = mybir.dt.int32
AF = mybir.ActivationFunctionType
ALU = mybir.AluOpType
AX = mybir.AxisListType

D = 1024
DFF = 2816
NFC = DFF // 128
NL = 4
EPS = 1e-6
NEG = -30000.0
ROPE_THETA = 500000.0
ENGS = ("sync", "scalar", "vector", "gpsimd", "tensor")
SAME_ENG_SYNC = True
DIFFDBG = int(os.environ.get('DIFFDBG', '0'))
DQ = int(os.environ.get('DQ', '0'))


class Sem:
    def __init__(self, h, name):
        self.h = h
        self.v = 0
        self.name = name


class Ev:
    __slots__ = ("sem", "val")

    def __init__(self, sem=None, val=None):
        self.sem = sem
        self.val = val


class Prog:
    def __init__(self, nc, stack):
        self.nc = nc
        self.stack = stack
        self.sems = {}
        self.nst = 0
        self.store_keys = []
        self.esem = {e: self.new_sem("ev_" + e) for e in ENGS}
        self.q = {e: [] for e in ENGS}
        self.lastw = {}
        self.readers = {}
        self.pending = {e: [] for e in ENGS}
        self.waited = {e: {} for e in ENGS}
        self.nops = 0

    def new_sem(self, name):
        name = re.sub(r"\d+_", "_", name, count=1) if name.count("_") >= 2 else name
        if name in self.sems:
            return self.sems[name]
        h = self.stack.enter_context(self.nc.semaphore(name))
        s = Sem(h, name)
        self.sems[name] = s
        return s

    def store(self, eng, fn, r, dsem):
        key = f"__st{self.nst}"
        self.nst += 1
        self.op(eng, fn, r=r, w=[key], dsem=dsem)
        self.store_keys.append(key)

    def drain(self, scratch_ap):
        keys = list(self.store_keys)
        self.store_keys = []
        self.op("gpsimd", lambda e: e.memset(scratch_ap, 0.0), r=keys, w=["__drain"])

    def op(self, eng, fn, r=(), w=(), ev=True, dsem=None):
        deps = []
        for b in r:
            e = self.lastw.get(b)
            if e is not None:
                deps.append(e)
        for b in w:
            e = self.lastw.get(b)
            if e is not None:
                deps.append(e)
            deps.extend(self.readers.get(b, ()))
        pend = self.pending[eng]
        if pend:
            deps = [d for d in deps if not (d.sem is None and any(d is p for p in pend))]
        n = 0
        if dsem is not None:
            dsem.v += 16
            event = Ev(dsem, dsem.v)
            n = 16
        elif ev:
            s = self.esem[eng]
            s.v += 1
            event = Ev(s, s.v)
            n = 1
            for p in self.pending[eng]:
                p.sem = s
                p.val = s.v
            self.pending[eng] = []
        else:
            event = Ev()
            self.pending[eng].append(event)
        self.q[eng].append((fn, deps, event, n))
        for b in w:
            self.lastw[b] = event
            self.readers[b] = []
        for b in r:
            self.readers.setdefault(b, []).append(event)
        self.nops += 1

    def flush(self):
        nc = self.nc
        for e in ENGS:
            assert not self.pending[e], f"pending events on {e}"
        with nc.Block() as block:
            for eng in ENGS:
                ops = self.q[eng]
                if not ops:
                    continue

                def body(e, ops=ops, eng=eng):
                    waited = self.waited[eng]
                    own = self.esem[eng]
                    for fn, deps, event, n in ops:
                        for d in deps:
                            s, v = d.sem, d.val
                            assert s is not None
                            if s is own and not SAME_ENG_SYNC:
                                continue
                            if waited.get(s, 0) >= v:
                                continue
                            e.wait_ge(s.h, v)
                            waited[s] = v
                        ins = fn(e)
                        if n:
                            ins.then_inc(event.sem.h, n)

                getattr(block, eng)(body)
        self.q = {e: [] for e in ENGS}


def build_nc(S, layers=NL, debug=False, nph=99):
    nt = S // 128
    nc = bass.Bass("TRN2", target_bir_lowering=False)
    okind = "ExternalOutput" if debug else "Internal"

    def din(name, shape, dt=F32):
        return nc.dram_tensor(name, list(shape), dt, kind="ExternalInput").ap()

    x_in = din("x", [S, D])
    pos_in = din("positions", [S], I32)
    attn_norm = din("attn_norm", [NL, D])
    w_in = din("w_in", [NL, D, 3 * D])
    w_out = din("w_out", [NL, D, D])
    q_norm = din("q_norm", [2, 64])
    k_norm = din("k_norm", [2, 64])
    lq1 = din("lambda_q1", [2, 64])
    lk1 = din("lambda_k1", [2, 64])
    lq2 = din("lambda_q2", [2, 64])
    lk2 = din("lambda_k2", [2, 64])
    sub_norm = din("sub_norm", [2, 128])
    ffn_norm = din("ffn_norm", [NL, D])
    w_gate = din("w_gate", [NL, D, DFF])
    w_up = din("w_up", [NL, D, DFF])
    w_down = din("w_down", [NL, DFF, D])
    out = nc.dram_tensor("out", [S, D], F32, kind="ExternalOutput").ap()
    xs = [nc.dram_tensor(f"xs{i}", [S, D], F32, kind=okind).ap() for i in range(2)]
    qT_d = nc.dram_tensor("qT_d", [D, S], BF16, kind=okind).ap()
    kT_d = nc.dram_tensor("kT_d", [D, S], BF16, kind=okind).ap()
    oT_d = nc.dram_tensor("oT_d", [D, S], BF16, kind=okind).ap()

    with ExitStack() as gs, nc.allow_low_precision("bf16 matmul operands, fp32 accumulation"):
        P = Prog(nc, gs)

        uid = [0]

        def sb(stack, name, shape, dt):
            uid[0] += 1
            return stack.enter_context(nc.sbuf_tensor(f"{name}_u{uid[0]}", list(shape), dt))

        def ps(stack, name, shape, dt):
            uid[0] += 1
            return stack.enter_context(nc.psum_tensor(f"{name}_u{uid[0]}", list(shape), dt))

        ident = sb(gs, "ident", [128, 128], BF16)
        negones = sb(gs, "negones", [128, 128], BF16)
        ntri = sb(gs, "ntri", [128, 128], BF16)
        maskneg = sb(gs, "maskneg", [128, 128], BF16)
        mask2 = sb(gs, "mask2", [128, 128], BF16)
        g_attn = sb(gs, "g_attn", [128, NL, 8], F32)
        g_ffn = sb(gs, "g_ffn", [128, NL, 8], F32)
        cosT = sb(gs, "cosT", [128, nt, 8], F32)
        sinT = sb(gs, "sinT", [128, nt, 8], F32)
        gq = sb(gs, "gq", [128, 2, 64], F32)
        gk = sb(gs, "gk", [128, 2, 64], F32)
        neglam = sb(gs, "neglam", [128, 2], F32)
        subg = sb(gs, "subg", [128, 2], F32)
        mpi = sb(gs, "mpi", [128, 1], F32)
        csem = P.new_sem("csem")
        ts = ExitStack()
        onesb = sb(ts, "onesb", [128, 128], BF16)
        negbig = sb(ts, "negbig", [128, 128], BF16)
        posi = sb(ts, "posi", [128, nt], I32)
        posf = sb(ts, "posf", [128, nt], F32)
        ang = sb(ts, "ang", [128, nt, 8], F32)
        ang2 = sb(ts, "ang2", [128, nt, 8], F32)
        lam_in = sb(ts, "lam_in", [128, 8, 64], F32)
        lam_tmp = sb(ts, "lam_tmp", [128, 4, 64], F32)
        lam_s = sb(ts, "lam_s", [128, 4], F32)
        lam_e = sb(ts, "lam_e", [128, 4], F32)

        V, G, A, T, SY = "vector", "gpsimd", "scalar", "tensor", "sync"

        P.op(G, lambda e: e.memset(onesb[:], 1.0), w=["onesb"])
        P.op(G, lambda e: e.memset(negones[:], -1.0), w=["negones"])
        P.op(G, lambda e: e.memset(negbig[:], NEG), w=["negbig"])
        P.op(G, lambda e: e.memset(mpi[:], -math.pi), w=["mpi"])
        P.op(G, lambda e: e.affine_select(out=ident[:], in_=onesb[:], pattern=[[1, 128]],
                                          compare_op=ALU.is_equal, fill=0.0, base=0,
                                          channel_multiplier=-1), r=["onesb"], w=["ident"])
        P.op(G, lambda e: e.affine_select(out=ntri[:], in_=negones[:], pattern=[[-1, 128]],
                                          compare_op=ALU.is_ge, fill=0.0, base=0,
                                          channel_multiplier=1), r=["negones"], w=["ntri"])
        P.op(G, lambda e: e.affine_select(out=maskneg[:], in_=negbig[:], pattern=[[-1, 128]],
                                          compare_op=ALU.is_ge, fill=0.0, base=0,
                                          channel_multiplier=1), r=["negbig"], w=["maskneg"])
        P.op(G, lambda e: e.memset(mask2[:], 0.0), w=["mask2"])
        P.op(G, lambda e: e.memset(mask2[64:128, 0:64], NEG), w=["mask2"])

        def small_dma(dst, src, key):
            P.op(SY, lambda e: e.dma_start(out=dst, in_=src), w=["consts"], dsem=csem)

        identF = sb(ts, "identF", [128, 128], F32)
        P.op(V, lambda e: e.tensor_copy(out=identF[:], in_=ident[:]), r=["ident"], w=["identF"])
        rows = sb(ts, "c_rows", [2 * NL * 8 + nt + 2, 128], F32)
        posr = sb(ts, "c_posr", [nt, 128], I32)
        NR = 2 * NL * 8
        small_dma(rows[0:NL * 8, :], attn_norm.rearrange("l (c p) -> (l c) p", p=128), "rows")
        small_dma(rows[NL * 8:NR, :], ffn_norm.rearrange("l (c p) -> (l c) p", p=128), "rows")
        small_dma(posr[:], pos_in.rearrange("(t p) -> t p", p=128), "posr")
        for j in range(2):
            small_dma(gq[:, j, :], q_norm[j, :].partition_broadcast(128), "gq")
            small_dma(gk[:, j, :], k_norm[j, :].partition_broadcast(128), "gk")
            for i, lt in enumerate((lq1, lk1, lq2, lk2)):
                small_dma(lam_in[:, j * 4 + i, :], lt[j, :].partition_broadcast(128), "lam_in")
        subr = sb(ts, "c_subr", [2, 128], F32)
        small_dma(subr[:], sub_norm, "subr")
        posrf = sb(ts, "c_posrf", [nt, 128], F32)
        P.op(V, lambda e: e.tensor_copy(out=posrf[:], in_=posr[:]), r=["consts"], w=["posrf"])
        with ExitStack() as cst:
            pc = ps(cst, "pc", [128, 512], F32)
            P.op(T, lambda e: e.transpose(out=pc[:, 0:NR], in_=rows[0:NR, :], identity=identF[0:NR, 0:NR]),
                 r=["consts", "identF"], w=["pc"])
            P.op(T, lambda e: e.transpose(out=pc[:, 64:64 + nt], in_=posrf[:], identity=identF[0:nt, 0:nt]),
                 r=["posrf", "identF"], w=["pc"])
            P.op(T, lambda e: e.transpose(out=pc[:, 128:130], in_=subr[:], identity=identF[0:2, 0:2]),
                 r=["consts", "identF"], w=["pc"])
            P.op(V, lambda e: e.tensor_copy(out=g_attn[:].rearrange("p l c -> p (l c)"), in_=pc[:, 0:NL * 8]), r=["pc"], w=["g_attn"])
            P.op(V, lambda e: e.tensor_copy(out=g_ffn[:].rearrange("p l c -> p (l c)"), in_=pc[:, NL * 8:NR]), r=["pc"], w=["g_ffn"])
            P.op(V, lambda e: e.tensor_copy(out=posf[:], in_=pc[:, 64:64 + nt]), r=["pc"], w=["posf"])
            P.op(V, lambda e: e.tensor_copy(out=subg[:], in_=pc[:, 128:130]), r=["pc"], w=["subg"])
            P.flush()

        inv_freq = (np.float32(ROPE_THETA) ** (-np.arange(0, 16, 2, dtype=np.float32) / np.float32(16))).astype(np.float32)
        for i in range(8):
            P.op(V, lambda e, i=i: e.tensor_scalar(out=ang[:, :, i], in0=posf[:], scalar1=float(inv_freq[i]),
                                                   scalar2=None, op0=ALU.mult), r=["posf"], w=["ang"], ev=(i == 7))
        twopi = 2.0 * math.pi
        C1 = 6.28125
        C2 = twopi - C1
        ki = sb(ts, "rope_ki", [128, nt, 8], I32)
        kf = sb(ts, "rope_kf", [128, nt, 8], F32)
        mk = sb(ts, "rope_mk", [128, nt, 8], F32)

        def vop(fn, r, w):
            P.op(V, fn, r=r, w=w)

        vop(lambda e: e.tensor_scalar(out=kf[:], in0=ang[:], scalar1=1.0 / twopi, scalar2=None, op0=ALU.mult), ["ang"], ["kf"])
        vop(lambda e: e.tensor_copy(out=ki[:], in_=kf[:]), ["kf"], ["ki"])
        vop(lambda e: e.tensor_copy(out=kf[:], in_=ki[:]), ["ki"], ["kf"])
        vop(lambda e: e.scalar_tensor_tensor(out=ang2[:], in0=kf[:], scalar=-C1, in1=ang[:], op0=ALU.mult, op1=ALU.add),
            ["kf", "ang"], ["ang2"])
        vop(lambda e: e.scalar_tensor_tensor(out=ang2[:], in0=kf[:], scalar=-C2, in1=ang2[:], op0=ALU.mult, op1=ALU.add),
            ["kf", "ang2"], ["ang2"])

        def wrap(buf, key):
            vop(lambda e: e.tensor_scalar(out=mk[:], in0=buf[:], scalar1=math.pi, scalar2=None, op0=ALU.is_gt), [key], ["mk"])
            vop(lambda e: e.scalar_tensor_tensor(out=buf[:], in0=mk[:], scalar=-twopi, in1=buf[:], op0=ALU.mult, op1=ALU.add),
                ["mk", key], [key])
            vop(lambda e: e.tensor_scalar(out=mk[:], in0=buf[:], scalar1=-math.pi, scalar2=None, op0=ALU.is_lt), [key], ["mk"])
            vop(lambda e: e.scalar_tensor_tensor(out=buf[:], in0=mk[:], scalar=twopi, in1=buf[:], op0=ALU.mult, op1=ALU.add),
                ["mk", key], [key])
            vop(lambda e: e.tensor_scalar(out=buf[:], in0=buf[:], scalar1=math.pi, scalar2=-math.pi, op0=ALU.min, op1=ALU.max),
                [key], [key])

        wrap(ang2, "ang2")
        P.op(A, lambda e: e.activation(out=sinT[:], in_=ang2[:], func=AF.Sin), r=["ang2"], w=["sinT"])
        vop(lambda e: e.tensor_scalar(out=ang[:], in0=ang2[:], scalar1=math.pi / 2, scalar2=None, op0=ALU.add), ["ang2", "ang"], ["ang"])
        wrap(ang, "ang")
        P.op(A, lambda e: e.activation(out=cosT[:], in_=ang[:], func=AF.Sin), r=["ang"], w=["cosT"])
        for j in range(2):
            for i in range(2):
                k = j * 2 + i
                P.op(V, lambda e, j=j, i=i, k=k: e.tensor_tensor(out=lam_tmp[:, k, :], in0=lam_in[:, j * 4 + 2 * i, :],
                                                               in1=lam_in[:, j * 4 + 2 * i + 1, :], op=ALU.mult),
                     r=["consts"], w=["lam_tmp"])
        P.op(V, lambda e: e.tensor_reduce(out=lam_s[:], in_=lam_tmp[:], axis=AX.X, op=ALU.add),
             r=["lam_tmp"], w=["lam_s"])
        P.op(A, lambda e: e.activation(out=lam_e[:], in_=lam_s[:], func=AF.Exp), r=["lam_s"], w=["lam_e"])
        for j in range(2):
            lam_init = 0.8 - 0.6 * math.exp(-0.3 * (2 * j + 1))
            P.op(V, lambda e, j=j: e.tensor_tensor(out=neglam[:, j:j + 1], in0=lam_e[:, 2 * j + 1:2 * j + 2],
                                                   in1=lam_e[:, 2 * j:2 * j + 1], op=ALU.subtract),
                 r=["lam_e"], w=["neglam"])
            P.op(V, lambda e, j=j, li=lam_init: e.tensor_scalar(out=neglam[:, j:j + 1], in0=neglam[:, j:j + 1],
                                                                scalar1=-li, scalar2=None, op0=ALU.add),
                 r=["neglam"], w=["neglam"])
            P.op(V, lambda e, j=j, li=lam_init: e.tensor_scalar(out=subg[:, j:j + 1], in0=subg[:, j:j + 1],
                                                                scalar1=(1.0 - li), scalar2=None, op0=ALU.mult),
                 r=["subg"], w=["subg"])
        if debug:
            dcos = nc.dram_tensor("dbg_cos", [128, nt * 8], F32, kind="ExternalOutput").ap()
            dsin = nc.dram_tensor("dbg_sin", [128, nt * 8], F32, kind="ExternalOutput").ap()
            dmisc = nc.dram_tensor("dbg_misc", [128, 4 + 2 * NL * 8], F32, kind="ExternalOutput").ap()
            P.store(SY, lambda e: e.dma_start(out=dcos, in_=cosT[:].rearrange("p t i -> p (t i)")), r=["cosT"], dsem=csem)
            P.store(SY, lambda e: e.dma_start(out=dsin, in_=sinT[:].rearrange("p t i -> p (t i)")), r=["sinT"], dsem=csem)
            P.store(SY, lambda e: e.dma_start(out=dmisc[:, 0:2], in_=neglam[:]), r=["neglam"], dsem=csem)
            P.store(SY, lambda e: e.dma_start(out=dmisc[:, 2:4], in_=subg[:]), r=["subg"], dsem=csem)
            P.store(SY, lambda e: e.dma_start(out=dmisc[:, 4:4 + NL * 8], in_=g_attn[:].rearrange("p l c -> p (l c)")), r=["g_attn"], dsem=csem)
            P.store(SY, lambda e: e.dma_start(out=dmisc[:, 4 + NL * 8:4 + 2 * NL * 8], in_=g_ffn[:].rearrange("p l c -> p (l c)")), r=["g_ffn"], dsem=csem)
            P.drain(mpi[:, 0:1])
        P.flush()
        ts.close()

        x_cur = x_in
        for l in range(layers):
            diff = (l % 2 == 1)
            jd = l // 2
            x_next = out if l == layers - 1 else xs[l % 2]
            with ExitStack() as ls:
                vw = 8 * 130 if diff else D
                v_sb = sb(ls, f"v_sb{l}", [128, nt, vw], BF16)
                if diff:
                    for h in range(8):
                        P.op(G, lambda e, h=h: e.memset(v_sb[:, :, h * 130 + 128:h * 130 + 130], 1.0), w=["v_sb"])
                if nph >= 1:
                    phase_qkv(nc, P, sb, ps, l, S, diff, jd, x_cur, w_in, g_attn, v_sb, qT_d, kT_d, ident,
                              gq, gk, cosT, sinT)
                if nph < 2:
                    pass
                elif diff:
                    phase_attn_diff(nc, P, sb, ps, l, S, jd, v_sb, qT_d, kT_d, oT_d, ident, mask2, neglam)
                else:
                    phase_attn_sb(nc, P, sb, ps, l, S, v_sb, qT_d, kT_d, oT_d, ident, ntri, negones, maskneg)
            if nph >= 3:
                phase_ffn(nc, P, sb, ps, l, S, diff, jd, x_cur, x_next, oT_d, w_out, w_gate, w_up, w_down,
                          g_ffn, subg, ident)
            x_cur = x_next
    return nc


def load_weight(P, sb_w, stage, ssem, dram_rows, nchunk, width, scale_ap_fn, key, step=1024, extra_r=("g_ffn", "subg")):
    nst = len(stage)
    k = 0
    for c in range(nchunk):
        for o in range(0, width, step):
            wdt = min(step, width - o)
            s = k % nst
            k += 1
            P.op("sync", lambda e, c=c, o=o, wdt=wdt, s=s: e.dma_start(out=stage[s][:, 0:wdt],
                                                                      in_=dram_rows(c)[:, o:o + wdt]),
                 w=[f"wst{s}"], dsem=ssem[s])
            sc = scale_ap_fn(c) if scale_ap_fn is not None else None
            eng = "vector" if (k % 2 == 0) else "scalar"
            dst = sb_w[:, c, o:o + wdt]
            srcap = stage[s][:, 0:wdt]
            rr = [f"wst{s}"] + (list(extra_r) if sc is not None else [])
            if eng == "vector":
                if sc is None:
                    P.op(eng, lambda e, dst=dst, srcap=srcap: e.tensor_copy(out=dst, in_=srcap), r=rr, w=[key])
                else:
                    P.op(eng, lambda e, dst=dst, srcap=srcap, sc=sc: e.tensor_scalar(out=dst, in0=srcap, scalar1=sc, scalar2=None,
                                                                                   op0=ALU.mult), r=rr, w=[key])
            else:
                if sc is None:
                    P.op(eng, lambda e, dst=dst, srcap=srcap: e.copy(out=dst, in_=srcap), r=rr, w=[key])
                else:
                    P.op(eng, lambda e, dst=dst, srcap=srcap, sc=sc: e.activation(out=dst, in_=srcap, func=AF.Copy, scale=sc),
                         r=rr, w=[key])


def rmsnorm_tile(P, xt_ap, xkey, junk, ss, rstd, hb, tag, n=D):
    P.op("scalar", lambda e: e.activation(out=junk, in_=xt_ap, func=AF.Square, accum_out=ss),
         r=[xkey], w=["junk" + tag, "ss" + tag])
    P.op("vector", lambda e: e.tensor_scalar(out=rstd, in0=ss, scalar1=1.0 / n, scalar2=EPS,
                                             op0=ALU.mult, op1=ALU.add), r=["ss" + tag], w=["rstd" + tag])
    P.op("scalar", lambda e: e.activation(out=rstd, in_=rstd, func=AF.Ln), r=["rstd" + tag], w=["rstd" + tag])
    P.op("scalar", lambda e: e.activation(out=rstd, in_=rstd, func=AF.Exp, scale=-0.5), r=["rstd" + tag], w=["rstd" + tag])
    P.op("vector", lambda e: e.tensor_scalar(out=hb, in0=xt_ap, scalar1=rstd, scalar2=None, op0=ALU.mult),
         r=[xkey, "rstd" + tag, "junk" + tag], w=["hb" + tag])


def phase_qkv(nc, P, sb, ps, l, S, diff, jd, x_cur, w_in, g_attn, v_sb, qT_d, kT_d, ident, gq, gk, cosT, sinT):
    nt = S // 128
    with ExitStack() as st:
        wq = sb(st, "wq", [128, 8, 3 * D], BF16)
        stage = [sb(st, f"wst{i}", [128, 1024], F32) for i in range(2)]
        ssem = [P.new_sem(f"qkv_ws{l}_{i}") for i in range(2)]
        xt = [sb(st, f"xt{i}", [128, D], F32) for i in range(2)]
        xsem = [P.new_sem(f"qkv_x{l}_{i}") for i in range(2)]
        junk = sb(st, "junkq", [128, D], BF16)
        ss = sb(st, "ssq", [128, 2], F32)
        rstd = sb(st, "rstdq", [128, 2], F32)
        hb = [sb(st, f"hbq{i}", [128, D], BF16) for i in range(2)]
        hTs = [sb(st, f"hT{i}", [128, 8, 128], BF16) for i in range(2)]
        qk_toks = [sb(st, f"qk_tok{i}", [128, 2 * D], BF16) for i in range(2)]
        qkT = [sb(st, f"qkT{i}", [128, 16, 512], BF16) for i in range(2)]
        osem = [P.new_sem(f"qkv_o{l}_{i}") for i in range(2)]
        if diff:
            qkf = sb(st, "qkf", [128, 2 * D], F32)
            sq2 = sb(st, "sq2", [128, 2 * D], F32)
            ssq = sb(st, "ssq2", [128, 32], F32)
            rs = sb(st, "rs", [128, 32], F32)
            rt = [sb(st, f"rt{i}", [128, 32, 8], F32) for i in range(4)]
        pqkv = [ps(st, f"pqkv{i}", [128, 512], F32) for i in range(6)]
        pT = [ps(st, f"pT{i}", [128, 8, 128], BF16) for i in range(2)]

        load_weight(P, wq, stage, ssem, lambda c: w_in[l, c * 128:(c + 1) * 128, :], 8, 3 * D,
                    lambda c: g_attn[:, l, c:c + 1], "wq", extra_r=["g_attn"])

        def front_a(tt):
            s = tt % 2
            hT = hTs[tt % 2]
            HT = f"hT{tt % 2}"
            P.op("sync", lambda e, tt=tt, s=s: e.dma_start(out=xt[s][:], in_=x_cur[tt * 128:(tt + 1) * 128, :]),
                 w=[f"xt{s}"], dsem=xsem[s])
            tag = f"q{s}"
            rmsnorm_tile(P, xt[s][:], f"xt{s}", junk[:], ss[:, s:s + 1], rstd[:, s:s + 1], hb[s][:], tag)
            for c in range(8):
                P.op("tensor", lambda e, c=c, s=s: e.transpose(out=pT[0][:, c, :], in_=hb[s][:, c * 128:(c + 1) * 128],
                                                              identity=ident[:]),
                     r=["hb" + tag, "ident"], w=["pT0"], ev=(c == 7))
            P.op("scalar", lambda e: e.copy(out=hT[:], in_=pT[0][:]), r=["pT0"], w=[HT])

        def front_b(tt):
            s = tt % 2
            j4 = tt % 4
            gi = tt // 4
            qs = gi % 2
            qk_tok = qk_toks[tt % 2]
            QK = f"qk_tok{tt % 2}"
            hT = hTs[tt % 2]
            HT = f"hT{tt % 2}"
            for r_ in range(6):
                for c in range(8):
                    P.op("tensor", lambda e, c=c, r_=r_: e.matmul(pqkv[r_][:], lhsT=hT[:, c, :],
                                                                  rhs=wq[:, c, r_ * 512:(r_ + 1) * 512],
                                                                  start=(c == 0), stop=(c == 7)),
                         r=[HT, "wq"], w=[f"pqkv{r_}"], ev=(c == 7))
            if not diff:
                for r_ in range(2):
                    P.op("scalar", lambda e, r_=r_: e.mul(out=qk_tok[:, r_ * 512:(r_ + 1) * 512], in_=pqkv[r_][:], mul=0.125),
                         r=[f"pqkv{r_}"], w=[QK])
                for r_ in range(2, 4):
                    P.op("vector", lambda e, r_=r_: e.tensor_copy(out=qk_tok[:, r_ * 512:(r_ + 1) * 512], in_=pqkv[r_][:]),
                         r=[f"pqkv{r_}"], w=[QK])
                P.op("scalar", lambda e, tt=tt: e.copy(out=v_sb[:, tt, 0:512], in_=pqkv[4][:]), r=["pqkv4"], w=["v_sb"])
                P.op("vector", lambda e, tt=tt: e.tensor_copy(out=v_sb[:, tt, 512:1024], in_=pqkv[5][:]),
                     r=["pqkv5"], w=["v_sb"])
            else:
                for r_ in range(4):
                    P.op("scalar", lambda e, r_=r_: e.activation(out=sq2[:, r_ * 512:(r_ + 1) * 512], in_=pqkv[r_][:],
                                                                 func=AF.Square), r=[f"pqkv{r_}"], w=["sq2", f"sqd{r_}"])
                for r_ in range(4):
                    gt = gq if r_ < 2 else gk
                    if DQ == 1:
                        P.op("vector", lambda e, r_=r_: e.tensor_copy(out=qkf[:, r_ * 512:(r_ + 1) * 512], in_=pqkv[r_][:]),
                             r=[f"pqkv{r_}", f"sqd{r_}", "consts"], w=["qkf"])
                        continue
                    P.op("vector", lambda e, r_=r_, gt=gt: e.tensor_tensor(
                        out=qkf[:, r_ * 512:(r_ + 1) * 512].rearrange("p (g d) -> p g d", d=64),
                        in0=pqkv[r_][:].rearrange("p (g d) -> p g d", d=64),
                        in1=gt[:, jd, :].unsqueeze(1).to_broadcast([128, 8, 64]), op=ALU.mult),
                         r=[f"pqkv{r_}", f"sqd{r_}", "consts"], w=["qkf"])
                P.op("vector", lambda e: e.tensor_reduce(out=ssq[:], in_=sq2[:].rearrange("p (g d) -> p g d", d=64),
                                                         axis=AX.X, op=ALU.add), r=["sq2"], w=["ssq"])
                P.op("vector", lambda e: e.tensor_scalar(out=rs[:], in0=ssq[:], scalar1=1.0 / 64, scalar2=EPS,
                                                         op0=ALU.mult, op1=ALU.add), r=["ssq"], w=["rs"])
                P.op("scalar", lambda e: e.activation(out=rs[:], in_=rs[:], func=AF.Ln), r=["rs"], w=["rs"])
                P.op("scalar", lambda e: e.activation(out=rs[:], in_=rs[:], func=AF.Exp, scale=-0.5), r=["rs"], w=["rs"])
                P.op("vector", lambda e: e.tensor_scalar(out=rs[:, 0:16], in0=rs[:, 0:16], scalar1=0.125, scalar2=None,
                                                         op0=ALU.mult), r=["rs"], w=["rs"])
                qv = qkf[:].rearrange("p (g d) -> p g d", d=64)
                qo = qk_tok[:].rearrange("p (g d) -> p g d", d=64)
                cb = cosT[:, tt, :].unsqueeze(1).to_broadcast([128, 32, 8])
                sbb = sinT[:, tt, :].unsqueeze(1).to_broadcast([128, 32, 8])
                x1 = qv[:, :, 0:8]
                x2 = qv[:, :, 8:16]
                P.op("vector", lambda e, x1=x1, cb=cb: e.tensor_tensor(out=rt[0][:], in0=x1, in1=cb, op=ALU.mult),
                     r=["qkf", "cosT"], w=["rt0"])
                P.op("vector", lambda e, x2=x2, sbb=sbb: e.tensor_tensor(out=rt[1][:], in0=x2, in1=sbb, op=ALU.mult),
                     r=["qkf", "sinT"], w=["rt1"])
                P.op("vector", lambda e, x2=x2, cb=cb: e.tensor_tensor(out=rt[2][:], in0=x2, in1=cb, op=ALU.mult),
                     r=["qkf", "cosT"], w=["rt2"])
                P.op("vector", lambda e, x1=x1, sbb=sbb: e.tensor_tensor(out=rt[3][:], in0=x1, in1=sbb, op=ALU.mult),
                     r=["qkf", "sinT"], w=["rt3"])
                if DQ != 2:
                    P.op("vector", lambda e, x1=x1: e.tensor_tensor(out=x1, in0=rt[0][:], in1=rt[1][:], op=ALU.subtract),
                         r=["rt0", "rt1", "rt3"], w=["qkf"])
                    P.op("vector", lambda e, x2=x2: e.tensor_tensor(out=x2, in0=rt[2][:], in1=rt[3][:], op=ALU.add),
                         r=["rt2", "rt3"], w=["qkf"])
                if DQ == 3:
                    P.op("vector", lambda e, qo=qo, qv=qv: e.tensor_copy(out=qo, in_=qv), r=["qkf", "rs"], w=[QK])
                else:
                    P.op("vector", lambda e, qo=qo, qv=qv: e.tensor_tensor(out=qo, in0=qv, in1=rs[:].unsqueeze(2).to_broadcast([128, 32, 64]),
                                                                           op=ALU.mult), r=["qkf", "rs"], w=[QK])
                vv = v_sb[:, tt, :].rearrange("p (h d) -> p h d", d=130)
                P.op("scalar", lambda e, vv=vv: e.copy(out=vv[:, 0:4, 0:128], in_=pqkv[4][:].rearrange("p (h d) -> p h d", d=128)),
                     r=["pqkv4"], w=["v_sb"])
                P.op("vector", lambda e, vv=vv: e.tensor_copy(out=vv[:, 4:8, 0:128], in_=pqkv[5][:].rearrange("p (h d) -> p h d", d=128)),
                     r=["pqkv5"], w=["v_sb"])
        def back(tt):
            s = tt % 2
            j4 = tt % 4
            gi = tt // 4
            qs = gi % 2
            qk_tok = qk_toks[tt % 2]
            QK = f"qk_tok{tt % 2}"
            for half in range(2):
                for c in range(8):
                    cc = half * 8 + c
                    P.op("tensor", lambda e, cc=cc, c=c: e.transpose(out=pT[1][:, c, :], in_=qk_tok[:, cc * 128:(cc + 1) * 128],
                                                                    identity=ident[:]),
                         r=[QK, "ident"], w=["pT1"], ev=(c == 7))
                eng = "scalar" if half == 0 else "vector"
                if eng == "scalar":
                    P.op(eng, lambda e, half=half, j4=j4, qs=qs: e.copy(out=qkT[qs][:, half * 8:(half + 1) * 8, j4 * 128:(j4 + 1) * 128],
                                                                        in_=pT[1][:]), r=["pT1"], w=[f"qkT{qs}"])
                else:
                    P.op(eng, lambda e, half=half, j4=j4, qs=qs: e.tensor_copy(out=qkT[qs][:, half * 8:(half + 1) * 8, j4 * 128:(j4 + 1) * 128],
                                                                               in_=pT[1][:]), r=["pT1"], w=[f"qkT{qs}"])
            if j4 == 3 or tt == nt - 1:
                t0 = gi * 512
                wdt = (j4 + 1) * 128
                P.store("gpsimd", lambda e, t0=t0, wdt=wdt, qs=qs: e.dma_start(
                    out=qT_d[:, t0:t0 + wdt].rearrange("(c p) t -> p c t", p=128), in_=qkT[qs][:, 0:8, 0:wdt]),
                        r=[f"qkT{qs}"], dsem=osem[qs])
                P.store("gpsimd", lambda e, t0=t0, wdt=wdt, qs=qs: e.dma_start(
                    out=kT_d[:, t0:t0 + wdt].rearrange("(c p) t -> p c t", p=128), in_=qkT[qs][:, 8:16, 0:wdt]),
                        r=[f"qkT{qs}"], dsem=osem[qs])
        for tt in range(nt + 1):
            if tt < nt:
                front_a(tt)
            if tt >= 1:
                back(tt - 1)
            if tt < nt:
                front_b(tt)
        P.drain(junk[:, 0:1])
        P.flush()


def phase_attn_sb(nc, P, sb, ps, l, S, v_sb, qT_d, kT_d, oT_d, ident, ntri, negones, maskneg):
    GW = min(1024, S)
    NTG = GW // 128
    ng = S // GW
    with ExitStack() as st:
        qTp = [sb(st, f"qTp{i}", [128, S], BF16) for i in range(2)]
        kTp = [sb(st, f"kTp{i}", [128, S], BF16) for i in range(2)]
        lsem = [P.new_sem(f"sb_ld{l}_{i}") for i in range(2)]
        Ebs = [sb(st, f"Eb{i}", [128, GW], F32) for i in range(2)]
        SPb = [sb(st, f"SPb{i}", [128, GW], BF16) for i in range(2)]
        R32 = [sb(st, f"R32_{i}", [128, GW], F32) for i in range(2)]
        Rb = [sb(st, f"Rb{i}", [128, GW], BF16) for i in range(2)]
        Wb = [sb(st, f"Wb{i}", [128, GW], BF16) for i in range(2)]
        oTs = [sb(st, f"oTs{i}", [64, GW], BF16) for i in range(2)]
        osem = [P.new_sem(f"sb_o{l}_{i}") for i in range(2)]
        Zb = ps(st, "Zb", [128, GW], F32)
        Lb = [ps(st, f"Lb{i}", [128, GW], F32) for i in range(2)]
        OT = ps(st, "OT", [128, GW], F32)

        iters = []
        gcount = 0
        for hp in range(8):
            for hh in range(2):
                for g in range(ng):
                    kbs = list(range(NTG * g + NTG - 1, -1, -1))
                    for idx, kb in enumerate(kbs):
                        iters.append(dict(hp=hp, hh=hh, g=g, kb=kb, first=(idx == 0), last=(kb == 0), gi=gcount,
                                          band=(kb - NTG * g) if kb >= NTG * g else -1))
                    gcount += 1
        for n, it in enumerate(iters):
            it["n"] = n
        loaded = set()
        ot_started = {}

        def ensure_loaded(hp):
            if hp in loaded or hp >= 8:
                return
            loaded.add(hp)
            s = hp % 2
            P.op("sync", lambda e: e.dma_start(out=qTp[s][:], in_=qT_d[hp * 128:(hp + 1) * 128, :]),
                 w=[f"qTp{s}"], dsem=lsem[s])
            P.op("sync", lambda e: e.dma_start(out=kTp[s][:], in_=kT_d[hp * 128:(hp + 1) * 128, :]),
                 w=[f"kTp{s}"], dsem=lsem[s])

        def cols(it):
            c0 = 128 * it["band"] if it["band"] >= 0 else 0
            return c0, GW

        def pieces(a, b):
            out = []
            while a < b:
                nb = min(b, (a // 512 + 1) * 512)
                out.append((a, nb))
                a = nb
            return out

        def mm_qk(it, dst, key, final_ev):
            n, hp, hh, g, kb = it["n"], it["hp"], it["hh"], it["g"], it["kb"]
            s = hp % 2
            pb = hh * 64
            c0, c1 = cols(it)
            t0 = g * GW
            band = it["band"] >= 0
            kT = kTp[s][pb:pb + 64, kb * 128:(kb + 1) * 128]
            pcs = pieces(c0, c1)
            for i, (a, b) in enumerate(pcs):
                lastp = (i == len(pcs) - 1)
                qT = qTp[s][pb:pb + 64, t0 + a:t0 + b]
                P.op("tensor", lambda e, a=a, b=b, qT=qT: e.matmul(dst[:, a:b], lhsT=kT, rhs=qT, start=True, stop=False,
                                                                  skip_group_check=True),
                     r=[f"qTp{s}", f"kTp{s}"], w=[key], ev=(final_ev and lastp and not band))
            if band:
                P.op("tensor", lambda e: e.matmul(dst[:, c0:c0 + 128], lhsT=ident[:], rhs=maskneg[:], start=False, stop=False,
                                                  skip_group_check=True),
                     r=["ident", "maskneg"], w=[key], ev=final_ev)

        def MM1(it):
            mm_qk(it, Zb, "Zb", True)

        def MM2(it):
            n = it["n"]
            mm_qk(it, Lb[n % 2], f"Lb{n % 2}", False)

        def EXP(it):
            n = it["n"]
            c0, c1 = cols(it)
            eb = Ebs[n % 2]
            P.op("scalar", lambda e: e.activation(out=eb[:, c0:c1], in_=Zb[:, c0:c1], func=AF.Exp),
                 r=["Zb"], w=[f"Eb{n % 2}"])

        def LN(it):
            n = it["n"]
            c0, c1 = cols(it)
            sp = SPb[n % 2]
            eb = Ebs[n % 2]
            P.op("scalar", lambda e: e.activation(out=sp[:, c0:c1], in_=eb[:, c0:c1], func=AF.Ln, bias=1.0, scale=1.0),
                 r=[f"Eb{n % 2}"], w=[f"SPb{n % 2}"])
            if it["last"]:
                return
            gi = it["gi"]
            r32 = R32[gi % 2]
            rb = Rb[n % 2]
            if it["band"] >= 0:
                P.op("vector", lambda e: e.tensor_copy(out=r32[:, c0:c0 + 128], in_=sp[:, c0:c0 + 128]),
                     r=[f"SPb{n % 2}"], w=[f"R32_{gi % 2}"])
                if c0 + 128 < GW:
                    P.op("vector", lambda e: e.tensor_tensor(out=r32[:, c0 + 128:GW], in0=r32[:, c0 + 128:GW],
                                                             in1=sp[:, c0 + 128:GW], op=ALU.add),
                         r=[f"SPb{n % 2}", f"R32_{gi % 2}"], w=[f"R32_{gi % 2}"])
            else:
                P.op("vector", lambda e: e.tensor_tensor(out=r32[:], in0=r32[:], in1=sp[:], op=ALU.add),
                     r=[f"SPb{n % 2}", f"R32_{gi % 2}"], w=[f"R32_{gi % 2}"])
            P.op("vector", lambda e: e.tensor_copy(out=rb[:, c0:c1], in_=r32[:, c0:c1]),
                 r=[f"R32_{gi % 2}"], w=[f"Rb{n % 2}"])

        def MM43(it):
            n = it["n"]
            c0, c1 = cols(it)
            lb = Lb[n % 2]
            sp = SPb[n % 2]
            if not it["first"]:
                r0 = c0 + 128 if it["band"] >= 0 else 0
                rb = Rb[(n - 1) % 2]
                for (a, b) in pieces(r0, c1):
                    P.op("tensor", lambda e, a=a, b=b: e.matmul(lb[:, a:b], lhsT=negones[:], rhs=rb[:, a:b], start=False, stop=False,
                                                                skip_group_check=True),
                         r=["negones", f"Rb{(n - 1) % 2}"], w=[f"Lb{n % 2}"], ev=False)
            pcs = pieces(c0, c1)
            for i, (a, b) in enumerate(pcs):
                P.op("tensor", lambda e, a=a, b=b: e.matmul(lb[:, a:b], lhsT=ntri[:], rhs=sp[:, a:b], start=False, stop=True,
                                                            skip_group_check=True),
                     r=["ntri", f"SPb{n % 2}"], w=[f"Lb{n % 2}"], ev=(i == len(pcs) - 1))

        def EXPW(it):
            n = it["n"]
            c0, c1 = cols(it)
            lb = Lb[n % 2]
            wb = Wb[n % 2]
            P.op("scalar", lambda e: e.activation(out=wb[:, c0:c1], in_=lb[:, c0:c1], func=AF.Exp),
                 r=[f"Lb{n % 2}"], w=[f"Wb{n % 2}"])

        def PV(it):
            n, hp, hh, g, kb, gi = it["n"], it["hp"], it["hh"], it["g"], it["kb"], it["gi"]
            h = 2 * hp + hh
            c0, c1 = cols(it)
            wb = Wb[n % 2]
            pcs = pieces(c0, c1)
            for i, (a, b) in enumerate(pcs):
                bank = a // 512
                st_ = (gi, bank) not in ot_started
                ot_started[(gi, bank)] = True
                P.op("tensor", lambda e, a=a, b=b, st_=st_: e.matmul(OT[0:64, a:b], lhsT=v_sb[:, kb, h * 64:(h + 1) * 64], rhs=wb[:, a:b],
                                                                    start=st_, stop=it["last"], skip_group_check=True),
                     r=["v_sb", f"Wb{n % 2}"], w=["OT"], ev=(i == len(pcs) - 1))
            if it["last"]:
                o = oTs[gi % 2]
                hw = GW // 2
                P.op("vector", lambda e: e.tensor_copy(out=o[:, 0:hw], in_=OT[0:64, 0:hw]), r=["OT"], w=[f"oTs{gi % 2}"])
                P.op("vector", lambda e: e.tensor_copy(out=o[:, hw:GW], in_=OT[0:64, hw:GW]), r=["OT"], w=[f"oTs{gi % 2}"])
                P.store("gpsimd", lambda e: e.dma_start(out=oT_d[h * 64:(h + 1) * 64, g * GW:(g + 1) * GW], in_=o[:]),
                        r=[f"oTs{gi % 2}"], dsem=osem[gi % 2])

        N = len(iters)
        ensure_loaded(0)

        def at(i):
            return iters[i] if 0 <= i < N else None

        for k in range(-1, N + 1):
            nxt, cur, prv = at(k + 1), at(k), at(k - 1)
            if nxt is not None and nxt["hh"] == 0 and nxt["g"] == 0 and nxt["first"]:
                ensure_loaded(nxt["hp"] + 1)
            if cur is not None:
                EXP(cur)
            if nxt is not None:
                MM1(nxt)
            if cur is not None:
                LN(cur)
                MM43(cur)
            if prv is not None:
                EXPW(prv)
                PV(prv)
            if nxt is not None:
                MM2(nxt)
        P.drain(Ebs[0][:, 0:1])
        P.flush()


def phase_attn_diff(nc, P, sb, ps, l, S, jd, v_sb, qT_d, kT_d, oT_d, ident, mask2, neglam):
    ng = S // 256
    with ExitStack() as st:
        qTp = [[sb(st, f"dqTp{i}{m}", [128, S], BF16) for m in range(2)] for i in range(2)]
        kTp = [sb(st, f"dkTp{i}", [128, S], BF16) for i in range(2)]
        lsem = [P.new_sem(f"df_ld{l}_{i}") for i in range(2)]
        for i in range(2):
            P.op("gpsimd", lambda e, i=i: e.memset(qTp[i][0][64:128, :], 0.0), w=[f"dqTp{i}"])
            P.op("gpsimd", lambda e, i=i: e.memset(qTp[i][1][0:64, :], 0.0), w=[f"dqTp{i}"])
        Pb = [sb(st, f"Pb{i}", [128, 512], BF16) for i in range(3)]
        accs = [sb(st, f"accs{i}", [128, 2, 2, 130], F32) for i in range(2)]
        rc = [sb(st, f"rc{i}", [128, 2, 2], F32) for i in range(2)]
        od = [sb(st, f"od{i}", [128, 2, 128], F32) for i in range(2)]
        junk = sb(st, "junkd", [128, 128], F32)
        ssd = [sb(st, f"ssd{i}", [128, 2], F32) for i in range(2)]
        rsd = [sb(st, f"rsd{i}", [128, 2], F32) for i in range(2)]
        ob = [sb(st, f"ob{i}", [128, 2, 128], BF16) for i in range(2)]
        oTd = [sb(st, f"oTd{i}", [128, 256], BF16) for i in range(2)]
        epi = []
        osem = [P.new_sem(f"df_o{l}_{i}") for i in range(2)]
        Sb = [ps(st, f"Sb{i}", [128, 512], F32) for i in range(2)]
        acc = [[ps(st, f"acc{j}{m}", [128, 512], F32) for m in range(2)] for j in range(2)]
        Tb = ps(st, "Tb", [128, 2, 128], BF16)

        iters = []
        gcount = 0
        for h in range(8):
            for g in range(ng):
                for kb in range(2 * g + 1, -1, -1):
                    iters.append(dict(h=h, g=g, kb=kb, last=(kb == 0), gi=gcount,
                                      band=(kb - 2 * g) if kb >= 2 * g else -1))
                gcount += 1
        for n, it in enumerate(iters):
            it["n"] = n
        loaded = set()

        def ensure_loaded(h):
            if h in loaded or h >= 8:
                return
            loaded.add(h)
            s = h % 2
            P.op("sync", lambda e: e.dma_start(out=qTp[s][0][0:64, :], in_=qT_d[h * 128:h * 128 + 64, :]),
                 w=[f"dqTp{s}"], dsem=lsem[s])
            P.op("sync", lambda e: e.dma_start(out=qTp[s][1][64:128, :], in_=qT_d[h * 128 + 64:(h + 1) * 128, :]),
                 w=[f"dqTp{s}"], dsem=lsem[s])
            P.op("sync", lambda e: e.dma_start(out=kTp[s][:], in_=kT_d[h * 128:(h + 1) * 128, :]),
                 w=[f"dkTp{s}"], dsem=lsem[s])

        def S1(it):
            n, h, g, kb = it["n"], it["h"], it["g"], it["kb"]
            s = h % 2
            c0 = 0
            dc = 128 * it["band"] if it["band"] >= 0 else 0
            t0 = g * 256
            sbk = Sb[n % 2]
            band = it["band"] >= 0
            for m in range(2):
                kT = kTp[s][:, kb * 128:(kb + 1) * 128]
                qT = qTp[s][m][:, t0 + c0:t0 + 256]
                P.op("tensor", lambda e, m=m, kT=kT, qT=qT: e.matmul(sbk[:, m * 256 + c0:(m + 1) * 256], lhsT=kT, rhs=qT,
                                                                     start=True, stop=(not band)),
                     r=[f"dqTp{s}", f"dkTp{s}"], w=[f"Sb{n % 2}"], ev=(m == 1 and not band))
                if band:
                    P.op("tensor", lambda e, m=m: e.matmul(sbk[:, m * 256 + dc:m * 256 + dc + 128], lhsT=ident[:], rhs=mask2[:],
                                                           start=False, stop=True),
                         r=["ident", "mask2"], w=[f"Sb{n % 2}"], ev=(m == 1))

        def S2(it):
            n = it["n"]
            c0 = 0
            sbk = Sb[n % 2]
            pb = Pb[n % 3]
            if c0 == 0:
                P.op("scalar", lambda e: e.activation(out=pb[:], in_=sbk[:], func=AF.Exp), r=[f"Sb{n % 2}"], w=[f"Pb{n % 3}"])
            else:
                for m in range(2):
                    P.op("scalar", lambda e, m=m: e.activation(out=pb[:, m * 256 + c0:(m + 1) * 256],
                                                               in_=sbk[:, m * 256 + c0:(m + 1) * 256], func=AF.Exp),
                         r=[f"Sb{n % 2}"], w=[f"Pb{n % 3}"])

        def S3(it):
            n, h, g, kb, gi = it["n"], it["h"], it["g"], it["kb"], it["gi"]
            j0 = it["band"] if it["band"] >= 0 else 0
            pb = Pb[n % 3]
            for j in range(j0, 2):
                for m in range(2):
                    P.op("tensor", lambda e, j=j, m=m: e.matmul(acc[j][m][:, 0:130], lhsT=pb[:, m * 256 + j * 128:m * 256 + (j + 1) * 128],
                                                                rhs=v_sb[:, kb, h * 130:(h + 1) * 130],
                                                                start=(kb == 2 * g + j), stop=(kb == 0)),
                         r=["v_sb", f"Pb{n % 3}"], w=[f"acc{j}{m}"], ev=(kb == 0 or (j == 1 and m == 1)))
            if not it["last"] or DIFFDBG >= 1:
                return
            b = gi % 2
            while epi and epi[0][0] <= gi - 2:
                epi.pop(0)[1]()
            A, R_, O, SS, RS, OB, OT_ = accs[b], rc[b], od[b], ssd[b], rsd[b], ob[b], oTd[b]
            ka, kr, ko, kss, krs, kob, kot = f"accs{b}", f"rc{b}", f"od{b}", f"ssd{b}", f"rsd{b}", f"ob{b}", f"oTd{b}"
            for j in range(2):
                for m in range(2):
                    if m == 0:
                        P.op("vector", lambda e, j=j, m=m: e.tensor_copy(out=A[:, j, m, :], in_=acc[j][m][:, 0:130]),
                             r=[f"acc{j}{m}"], w=[ka])
                    else:
                        P.op("scalar", lambda e, j=j, m=m: e.copy(out=A[:, j, m, :], in_=acc[j][m][:, 0:130]),
                             r=[f"acc{j}{m}"], w=[ka])
            epi.append((gi, lambda: P.op("vector", lambda e: e.reciprocal(out=R_[:], in_=A[:, :, :, 128]), r=[ka], w=[kr])))
            epi.append((gi, lambda: P.op("vector", lambda e: e.tensor_tensor(out=R_[:, :, 1], in0=R_[:, :, 1],
                                                                       in1=neglam[:, jd:jd + 1].to_broadcast([128, 2]), op=ALU.mult),
                                    r=[kr, "neglam"], w=[kr])))
            for j in range(2):
                epi.append((gi, lambda j=j: P.op("vector", lambda e: e.tensor_scalar(out=O[:, j, :], in0=A[:, j, 0, 0:128], scalar1=R_[:, j, 0:1],
                                                                                scalar2=None, op0=ALU.mult), r=[ka, kr], w=[ko])))
                epi.append((gi, lambda j=j: P.op("vector", lambda e: e.scalar_tensor_tensor(out=O[:, j, :], in0=A[:, j, 1, 0:128],
                                                                                       scalar=R_[:, j, 1:2], in1=O[:, j, :],
                                                                                       op0=ALU.mult, op1=ALU.add), r=[ka, kr, ko], w=[ko])))
                epi.append((gi, lambda j=j: P.op("scalar", lambda e: e.activation(out=junk[:], in_=O[:, j, :], func=AF.Square,
                                                                             accum_out=SS[:, j:j + 1]), r=[ko], w=["junkd", kss])))
            epi.append((gi, lambda: P.op("vector", lambda e: e.tensor_scalar(out=RS[:], in0=SS[:], scalar1=1.0 / 128, scalar2=EPS,
                                                                        op0=ALU.mult, op1=ALU.add), r=[kss], w=[krs])))
            epi.append((gi, lambda: P.op("scalar", lambda e: e.activation(out=RS[:], in_=RS[:], func=AF.Ln), r=[krs], w=[krs])))
            epi.append((gi, lambda: P.op("scalar", lambda e: e.activation(out=RS[:], in_=RS[:], func=AF.Exp, scale=-0.5), r=[krs], w=[krs])))
            for j in range(2):
                epi.append((gi, lambda j=j: P.op("vector", lambda e: e.tensor_scalar(out=OB[:, j, :], in0=O[:, j, :], scalar1=RS[:, j:j + 1],
                                                                                scalar2=None, op0=ALU.mult), r=[ko, krs], w=[kob])))
            for j in range(2):
                epi.append((gi, lambda j=j: P.op("tensor", lambda e: e.transpose(out=Tb[:, j, :], in_=OB[:, j, :], identity=ident[:]),
                                            r=[kob, "ident"], w=["Tb"])))
            epi.append((gi, lambda: P.op("vector", lambda e: e.tensor_copy(out=OT_[:], in_=Tb[:].rearrange("p j t -> p (j t)")),
                                    r=["Tb"], w=[kot])))
            epi.append((gi, lambda: P.store("gpsimd", lambda e: e.dma_start(out=oT_d[h * 128:(h + 1) * 128, g * 256:(g + 1) * 256], in_=OT_[:]),
                                       r=[kot], dsem=osem[b])))

        N = len(iters)
        ensure_loaded(0)
        for step in range(N + 1):
            if step < N:
                it = iters[step]
                if it["g"] == 0 and it["kb"] == 1:
                    ensure_loaded(it["h"] + 1)
                if DIFFDBG < 4:
                    S1(it)
                if DIFFDBG < 3:
                    S2(it)
            if 0 <= step - 1 < N and DIFFDBG < 2:
                S3(iters[step - 1])
            for _ in range(2):
                if epi:
                    epi.pop(0)[1]()
        while epi:
            epi.pop(0)[1]()
        P.drain(junk[:, 0:1])
        P.flush()


def phase_ffn(nc, P, sb, ps, l, S, diff, jd, x_cur, x_next, oT_d, w_out, w_gate, w_up, w_down, g_ffn, subg, ident):
    GW = 256
    ng = S // GW
    nj = GW // 128
    with ExitStack() as st:
        wo = sb(st, "wo", [128, 8, D], BF16)
        wg = sb(st, "wg", [128, 8, DFF], BF16)
        wu = sb(st, "wu", [128, 8, DFF], BF16)
        wd = sb(st, "wd", [128, NFC, D], BF16)
        with ExitStack() as st2:
            stage = [sb(st2, f"fwst{i}", [128, 1024], F32) for i in range(4)]
            ssem = [P.new_sem(f"ffn_ws{l}_{i}") for i in range(4)]
            load_weight(P, wo, stage, ssem, lambda c: w_out[l, c * 128:(c + 1) * 128, :], 8, D,
                        (lambda c: subg[:, jd:jd + 1]) if diff else None, "wo")
            load_weight(P, wg, stage, ssem, lambda c: w_gate[l, c * 128:(c + 1) * 128, :], 8, DFF,
                        lambda c: g_ffn[:, l, c:c + 1], "wg")
            load_weight(P, wu, stage, ssem, lambda c: w_up[l, c * 128:(c + 1) * 128, :], 8, DFF,
                        lambda c: g_ffn[:, l, c:c + 1], "wu")
            load_weight(P, wd, stage, ssem, lambda c: w_down[l, c * 128:(c + 1) * 128, :], NFC, D, None, "wd")
            P.flush()
        oTs = [sb(st, f"foT{i}", [128, 8, GW], BF16) for i in range(2)]
        olsem = [P.new_sem(f"ffn_ol{l}_{i}") for i in range(2)]
        xg = [[sb(st, f"xg{s}{j}", [128, D], F32) for j in range(nj)] for s in range(2)]
        xlsem = [[P.new_sem(f"ffn_xl{l}_{s}{j}") for j in range(nj)] for s in range(2)]
        xssem = [[P.new_sem(f"ffn_xs{l}_{s}{j}") for j in range(nj)] for s in range(2)]
        junk = sb(st, "junkf", [128, D], BF16)
        ss = sb(st, "ssf", [128, 2], F32)
        rstd = sb(st, "rstdf", [128, 2], F32)
        hb = [sb(st, f"hbf{i}", [128, D], BF16) for i in range(2)]
        h2T = sb(st, "h2T", [128, 8, GW], BF16)
        aT = sb(st, "aT", [128, NFC, GW], BF16)
        sg = [sb(st, f"sg{i}", [128, GW], F32) for i in range(2)]
        pO = [ps(st, f"pO{i}", [128, 512], F32) for i in range(1)]
        pT = ps(st, "pTf", [128, 8, 128], BF16)
        pG = [ps(st, f"pG{i}", [128, GW], F32) for i in range(2)]
        pU = [ps(st, f"pU{i}", [128, GW], F32) for i in range(2)]
        pD = [ps(st, f"pD{i}", [128, 512], F32) for i in range(2)]

        kO = 0
        kD = 0
        kh = 0
        for g in range(ng):
            s = g % 2
            t0 = g * GW
            P.op("sync", lambda e, s=s, t0=t0: e.dma_start(out=oTs[s][:], in_=oT_d[:, t0:t0 + GW].rearrange("(c p) t -> p c t", p=128)),
                 w=[f"foT{s}"], dsem=olsem[s])
            for j in range(nj):
                P.op("sync", lambda e, s=s, j=j, t0=t0: e.dma_start(out=xg[s][j][:], in_=x_cur[t0 + j * 128:t0 + (j + 1) * 128, :]),
                     w=[f"xg{s}{j}"], dsem=xlsem[s][j])
            for j in range(nj):
                for half in range(2):
                    po = pO[0]
                    pk = "pO0"
                    kO += 1
                    for c in range(8):
                        P.op("tensor", lambda e, c=c, j=j, half=half, po=po, s=s: e.matmul(
                            po[:], lhsT=oTs[s][:, c, j * 128:(j + 1) * 128], rhs=wo[:, c, half * 512:(half + 1) * 512],
                            start=(c == 0), stop=(c == 7)), r=[f"foT{s}", "wo"], w=[pk], ev=(c == 7))
                    xa = xg[s][j][:, half * 512:(half + 1) * 512]
                    P.op("vector", lambda e, xa=xa, po=po: e.tensor_tensor(out=xa, in0=xa, in1=po[:], op=ALU.add),
                         r=[pk, f"xg{s}{j}"], w=[f"xg{s}{j}"])
            for j in range(nj):
                hs = kh % 2
                kh += 1
                tag = f"f{hs}"
                rmsnorm_tile(P, xg[s][j][:], f"xg{s}{j}", junk[:], ss[:, hs:hs + 1], rstd[:, hs:hs + 1], hb[hs][:], tag)
                for c in range(8):
                    P.op("tensor", lambda e, c=c, hs=hs: e.transpose(out=pT[:, c, :], in_=hb[hs][:, c * 128:(c + 1) * 128],
                                                                    identity=ident[:]),
                         r=["hb" + tag, "ident"], w=["pTf"], ev=(c == 7))
                P.op("scalar", lambda e, j=j: e.copy(out=h2T[:, :, j * 128:(j + 1) * 128], in_=pT[:]), r=["pTf"], w=["h2T"])
            for fc in range(NFC):
                b = fc % 2
                for c in range(8):
                    P.op("tensor", lambda e, c=c, fc=fc, b=b: e.matmul(pG[b][:], lhsT=wg[:, c, fc * 128:(fc + 1) * 128], rhs=h2T[:, c, :],
                                                                      start=(c == 0), stop=(c == 7)),
                         r=["wg", "h2T"], w=[f"pG{b}"], ev=(c == 7))
                for c in range(8):
                    P.op("tensor", lambda e, c=c, fc=fc, b=b: e.matmul(pU[b][:], lhsT=wu[:, c, fc * 128:(fc + 1) * 128], rhs=h2T[:, c, :],
                                                                      start=(c == 0), stop=(c == 7)),
                         r=["wu", "h2T"], w=[f"pU{b}"], ev=(c == 7))
                P.op("scalar", lambda e, b=b: e.activation(out=sg[b][:], in_=pG[b][:], func=AF.Silu), r=[f"pG{b}"], w=[f"sg{b}"])
                P.op("vector", lambda e, b=b, fc=fc: e.tensor_tensor(out=aT[:, fc, :], in0=sg[b][:], in1=pU[b][:], op=ALU.mult),
                     r=[f"sg{b}", f"pU{b}"], w=["aT"])
            for j in range(nj):
                for half in range(2):
                    pd = pD[kD % 2]
                    pk = f"pD{kD % 2}"
                    kD += 1
                    for fc in range(NFC):
                        P.op("tensor", lambda e, fc=fc, j=j, half=half, pd=pd: e.matmul(
                            pd[:], lhsT=aT[:, fc, j * 128:(j + 1) * 128], rhs=wd[:, fc, half * 512:(half + 1) * 512],
                            start=(fc == 0), stop=(fc == NFC - 1)), r=["aT", "wd"], w=[pk], ev=(fc == NFC - 1))
                    xa = xg[s][j][:, half * 512:(half + 1) * 512]
                    P.op("vector", lambda e, xa=xa, pd=pd: e.tensor_tensor(out=xa, in0=xa, in1=pd[:], op=ALU.add),
                         r=[pk, f"xg{s}{j}"], w=[f"xg{s}{j}"])
                P.store("gpsimd", lambda e, s=s, j=j, t0=t0: e.dma_start(out=x_next[t0 + j * 128:t0 + (j + 1) * 128, :], in_=xg[s][j][:]),
                        r=[f"xg{s}{j}"], dsem=xssem[s][j])
        P.drain(junk[:, 0:1])
        P.flush()


_NC_CACHE = {}


def _get_nc(S, layers=NL, debug=False):
    key = (S, layers, debug)
    if key not in _NC_CACHE:
        _NC_CACHE[key] = build_nc(S, layers, debug)
    return _NC_CACHE[key]


def kernel(x, positions, attn_norm, w_in, w_out, q_norm, k_norm, lambda_q1, lambda_k1,
           lambda_q2, lambda_k2, sub_norm, ffn_norm, w_gate, w_up, w_down):
    x = np.asarray(x)
    B, S, _ = x.shape
    nc = _get_nc(S)
    shared = dict(attn_norm=attn_norm, w_in=w_in, w_out=w_out, q_norm=q_norm, k_norm=k_norm,
                  lambda_q1=lambda_q1, lambda_k1=lambda_k1, lambda_q2=lambda_q2, lambda_k2=lambda_k2,
                  sub_norm=sub_norm, ffn_norm=ffn_norm, w_gate=w_gate, w_up=w_up, w_down=w_down)
    shared = {k: np.ascontiguousarray(np.asarray(v, dtype=np.float32)) for k, v in shared.items()}
    in_maps = []
    for b in range(B):
        m = dict(shared)
        m["x"] = np.ascontiguousarray(x[b], dtype=np.float32)
        m["positions"] = np.ascontiguousarray(np.asarray(positions)[b], dtype=np.int32)
        in_maps.append(m)
    res = run_bass_kernel_spmd(nc, in_maps, core_ids=list(range(B)))
    return np.stack([np.asarray(r["out"]) for r in res.results], axis=0).astype(np.float32)
```
